# Optimizing a Trainium2 kernel written in Bass

```python
import jax, jax.numpy as jnp
from jax import lax
import numpy as np

D_MODEL = 1024
BATCH = 4
SEQ = 4096
DEPTH = 4
DEC_BATCH = 128
DEC_SEQ = 8
PAST_LEN = 2048
PAGE_SIZE = 128

A_WIDTH = D_MODEL // 2
B_WIDTH = D_MODEL - A_WIDTH
HEAD_DIM = 64
FOX_HEADS = A_WIDTH // HEAD_DIM
Q_BLOCK = 128
FORGET_BIAS = 3.0
RG_BLOCKS = 8
RG_BD = B_WIDTH // RG_BLOCKS
RG_CONV = 4
RG_C = 8.0
POOL_WINDOWS = (2, 4, 8, 16)
POOL_GROUPS = len(POOL_WINDOWS)
POOL_GD = D_MODEL // POOL_GROUPS
POOL_BUF = max(POOL_WINDOWS) - 1
D_FF = 3 * D_MODEL
FFN_CONV = 3
N_AB_LAYERS = (DEPTH + 1) // 2
N_POOL_LAYERS = DEPTH // 2
IN_SPLITS = (A_WIDTH, 2 * A_WIDTH, 3 * A_WIDTH, 3 * A_WIDTH + FOX_HEADS, 3 * A_WIDTH + FOX_HEADS + B_WIDTH)
IN_WIDTH = 3 * A_WIDTH + FOX_HEADS + 2 * B_WIDTH
EPS = 1e-6
NEG = -1e30

kernel_name = "fox_rglru_pool_convffn_step"


def _rmsnorm(x, g):
    xf = x.astype(jnp.float32)
    y = xf * lax.rsqrt(jnp.mean(xf * xf, axis=-1, keepdims=True) + EPS)
    return (y * g.astype(jnp.float32)).astype(x.dtype)


def _causal_dwconv(u, prev, w, b):
    width, t = w.shape[0], u.shape[1]
    ue = jnp.concatenate([prev.astype(u.dtype), u], axis=1)
    y = b
    for j in range(width):
        y = y + ue[:, j:j + t] * w[j]
    return y, ue[:, t:]


def _gather_pages(pool, page_table):
    pages = pool[page_table]
    return pages.reshape((pages.shape[0], pages.shape[1] * pages.shape[2]) + pages.shape[3:])


def _fox_attention(q, k, v, c_q, c_k, q_pos, k_pos):
    bsz, tq, nh, dh = q.shape
    qb = min(Q_BLOCK, tq)
    nb = -(-tq // qb)
    pad = nb * qb - tq
    if pad:
        q = jnp.pad(q, ((0, 0), (0, pad), (0, 0), (0, 0)))
        c_q = jnp.pad(c_q, ((0, 0), (0, pad), (0, 0)), mode="edge")
        q_pos = jnp.pad(q_pos, (0, pad), mode="edge")
    q_blk = q.reshape(bsz, nb, qb, nh, dh).swapaxes(0, 1)
    c_blk = c_q.reshape(bsz, nb, qb, nh).swapaxes(0, 1)
    p_blk = q_pos.reshape(nb, qb)
    c_k_h = jnp.swapaxes(c_k, 1, 2)
    scale = HEAD_DIM ** -0.5

    def one_block(args):
        qi, ci, pi = args
        s = jnp.einsum("bqhd,bkhd->bhqk", qi, k).astype(jnp.float32) * scale
        s = s + jnp.swapaxes(ci, 1, 2)[..., None] - c_k_h[:, :, None, :]
        mask = k_pos[None, :] <= pi[:, None]
        s = jnp.where(mask[None, None], s, NEG)
        prob = jax.nn.softmax(s, axis=-1)
        return jnp.einsum("bhqk,bkhd->bqhd", prob.astype(v.dtype), v)

    out = lax.map(one_block, (q_blk, c_blk, p_blk))
    return out.swapaxes(0, 1).reshape(bsz, nb * qb, nh, dh)[:, :tq]


def _rglru(xc, h0, w_a, b_a, w_x, b_x, lam):
    bsz, t, c = xc.shape
    xb = xc.reshape(bsz, t, RG_BLOCKS, RG_BD)
    r = jax.nn.sigmoid((jnp.einsum("bthi,hij->bthj", xb, w_a).reshape(bsz, t, c) + b_a).astype(jnp.float32))
    i = jax.nn.sigmoid((jnp.einsum("bthi,hij->bthj", xb, w_x).reshape(bsz, t, c) + b_x).astype(jnp.float32))
    log_a = -RG_C * r * jax.nn.softplus(-lam.astype(jnp.float32))
    a = jnp.exp(log_a)
    inp = jnp.sqrt(-jnp.expm1(2.0 * log_a)) * i * xc.astype(jnp.float32)

    def combine(lhs, rhs):
        return (lhs[0] * rhs[0], rhs[0] * lhs[1] + rhs[1])

    a_cum, h_zero = lax.associative_scan(combine, (a, inp), axis=1)
    h = h_zero + a_cum * h0.astype(jnp.float32)[:, None, :]
    return h.astype(xc.dtype), h[:, -1].astype(xc.dtype)


def _mixer_ab(xn, past, h0, conv_prev, p, li):
    bsz, t, _ = xn.shape
    proj = jnp.einsum("btd,de->bte", xn, p["ab_w_in"][li])
    q, k, v, f, xr, gate = jnp.split(proj, IN_SPLITS, axis=-1)
    q = _rmsnorm(q.reshape(bsz, t, FOX_HEADS, HEAD_DIM), p["ab_q_gain"][li])
    k = _rmsnorm(k.reshape(bsz, t, FOX_HEADS, HEAD_DIM), p["ab_k_gain"][li])
    v = v.reshape(bsz, t, FOX_HEADS, HEAD_DIM)
    logf = jax.nn.log_sigmoid((f + p["ab_b_f"][li]).astype(jnp.float32))
    if past is None:
        past_len = 0
        k_all, v_all, lf_all = k, v, logf
    else:
        k_past, v_past, lf_past = past
        past_len = k_past.shape[1]
        k_all = jnp.concatenate([k_past.astype(k.dtype), k], axis=1)
        v_all = jnp.concatenate([v_past.astype(v.dtype), v], axis=1)
        lf_all = jnp.concatenate([lf_past.astype(jnp.float32), logf], axis=1)
    c_all = jnp.cumsum(lf_all, axis=1)
    k_pos = jnp.arange(past_len + t, dtype=jnp.int32)
    attn = _fox_attention(q, k_all, v_all, c_all[:, past_len:], c_all, k_pos[past_len:], k_pos)
    xc, conv_new = _causal_dwconv(xr, conv_prev, p["ab_conv_w"][li], p["ab_conv_b"][li])
    h, h_last = _rglru(xc, h0, p["ab_w_a"][li], p["ab_b_a"][li], p["ab_w_x"][li], p["ab_b_x"][li], p["ab_lambda"][li])
    y_rg = h * jax.nn.gelu(gate)
    merged = jnp.concatenate([attn.reshape(bsz, t, A_WIDTH), y_rg], axis=-1)
    out = jnp.einsum("bte,ed->btd", merged, p["ab_w_out"][li])
    return out, (k, v, logf.astype(xn.dtype), h_last, conv_new)


def _pool_mixer(xn, pos0, prev, w_grp, scale):
    bsz, t, d = xn.shape
    xe = jnp.concatenate([prev.astype(xn.dtype), xn], axis=1)
    cs = jnp.concatenate([jnp.zeros((bsz, 1, d), jnp.float32), jnp.cumsum(xe.astype(jnp.float32), axis=1)], axis=1)
    pos = pos0 + jnp.arange(t, dtype=jnp.int32)
    outs = []
    for g, w in enumerate(POOL_WINDOWS):
        lo, hi = g * POOL_GD, (g + 1) * POOL_GD
        end = cs[:, POOL_BUF + 1:POOL_BUF + 1 + t, lo:hi]
        start = cs[:, POOL_BUF + 1 - w:POOL_BUF + 1 - w + t, lo:hi]
        cnt = jnp.minimum(pos + 1, w).astype(jnp.float32)[None, :, None]
        diff = (end - start) / cnt - xn[:, :, lo:hi].astype(jnp.float32)
        outs.append(jnp.einsum("btc,ce->bte", diff.astype(xn.dtype), w_grp[g]))
    y = jnp.concatenate(outs, axis=-1) * scale
    return y.astype(xn.dtype), xe[:, t:]


def _conv_ffn(xn, prev, w_up, cw, cb, w_down):
    u = jnp.einsum("btd,df->btf", xn, w_up)
    uc, new_prev = _causal_dwconv(u, prev, cw, cb)
    g, val = jnp.split(uc, 2, axis=-1)
    return jnp.einsum("btf,fd->btd", jax.nn.gelu(g) * val, w_down), new_prev


def _trunk(x, pos0, p, paged, rg_h, rg_conv, pool_buf, ffn_buf):
    ks, vs, lfs, hs, cs, pbs, fbs = [], [], [], [], [], [], []
    for layer in range(DEPTH):
        li = layer // 2
        xn = _rmsnorm(x, p["norm_mix"][layer])
        if layer % 2 == 0:
            past = None
            if paged is not None:
                ck, cv, cf, pt = paged
                past = (_gather_pages(ck[li], pt), _gather_pages(cv[li], pt), _gather_pages(cf[li], pt))
            y, (k, v, lf, h_last, conv_new) = _mixer_ab(xn, past, rg_h[li], rg_conv[li], p, li)
            ks.append(k); vs.append(v); lfs.append(lf); hs.append(h_last); cs.append(conv_new)
        else:
            y, pb = _pool_mixer(xn, pos0, pool_buf[li], p["pool_w"][li], p["pool_scale"][li])
            pbs.append(pb)
        x = x + y
        xn = _rmsnorm(x, p["norm_ffn"][layer])
        y, fb = _conv_ffn(xn, ffn_buf[layer], p["ffn_w_up"][layer], p["ffn_conv_w"][layer],
                          p["ffn_conv_b"][layer], p["ffn_w_down"][layer])
        fbs.append(fb)
        x = x + y
    return x, (jnp.stack(ks), jnp.stack(vs), jnp.stack(lfs), jnp.stack(hs), jnp.stack(cs),
               jnp.stack(pbs), jnp.stack(fbs))


def setup_inputs(seed: int = 0) -> dict:
    key = jax.random.key(seed)
    keys = iter(jax.random.split(key, 40))
    f32 = jnp.float32

    def nrm(shape, s=1.0):
        return jax.random.normal(next(keys), shape, f32) * s

    n_pages = PAST_LEN // PAGE_SIZE
    n_used = DEC_BATCH * n_pages
    n_pool_pages = n_used + max(1, n_used // 4)
    perm = jax.random.permutation(next(keys), n_pool_pages)
    page_table = perm[:n_used].reshape(DEC_BATCH, n_pages).astype(jnp.int32)

    x_prompt = nrm((BATCH, SEQ, D_MODEL))
    x_sample = nrm((DEC_BATCH, DEC_SEQ, D_MODEL))
    cache_k = nrm((N_AB_LAYERS, n_pool_pages, PAGE_SIZE, FOX_HEADS, HEAD_DIM))
    cache_v = nrm((N_AB_LAYERS, n_pool_pages, PAGE_SIZE, FOX_HEADS, HEAD_DIM))
    cache_logf = jax.nn.log_sigmoid(FORGET_BIAS + nrm((N_AB_LAYERS, n_pool_pages, PAGE_SIZE, FOX_HEADS)))
    state_rg_h = nrm((N_AB_LAYERS, DEC_BATCH, B_WIDTH), 0.5)
    state_rg_conv = nrm((N_AB_LAYERS, DEC_BATCH, RG_CONV - 1, B_WIDTH))
    state_pool = nrm((N_POOL_LAYERS, DEC_BATCH, POOL_BUF, D_MODEL))
    state_ffn_conv = nrm((DEPTH, DEC_BATCH, FFN_CONV - 1, 2 * D_FF))

    a0 = jax.random.uniform(next(keys), (N_AB_LAYERS, B_WIDTH), f32, minval=0.9, maxval=0.999)
    return {
        "x_prompt": x_prompt,
        "x_sample": x_sample,
        "cache_k": cache_k,
        "cache_v": cache_v,
        "cache_logf": cache_logf,
        "state_rg_h": state_rg_h,
        "state_rg_conv": state_rg_conv,
        "state_pool": state_pool,
        "state_ffn_conv": state_ffn_conv,
        "page_table": page_table,
        "norm_mix": 1.0 + nrm((DEPTH, D_MODEL), 0.02),
        "norm_ffn": 1.0 + nrm((DEPTH, D_MODEL), 0.02),
        "ab_w_in": nrm((N_AB_LAYERS, D_MODEL, IN_WIDTH), D_MODEL ** -0.5),
        "ab_b_f": FORGET_BIAS + nrm((N_AB_LAYERS, FOX_HEADS), 0.1),
        "ab_q_gain": 1.0 + nrm((N_AB_LAYERS, HEAD_DIM), 0.02),
        "ab_k_gain": 1.0 + nrm((N_AB_LAYERS, HEAD_DIM), 0.02),
        "ab_conv_w": nrm((N_AB_LAYERS, RG_CONV, B_WIDTH), RG_CONV ** -0.5),
        "ab_conv_b": nrm((N_AB_LAYERS, B_WIDTH), 0.02),
        "ab_w_a": nrm((N_AB_LAYERS, RG_BLOCKS, RG_BD, RG_BD), RG_BD ** -0.5),
        "ab_b_a": nrm((N_AB_LAYERS, B_WIDTH), 0.02),
        "ab_w_x": nrm((N_AB_LAYERS, RG_BLOCKS, RG_BD, RG_BD), RG_BD ** -0.5),
        "ab_b_x": nrm((N_AB_LAYERS, B_WIDTH), 0.02),
        "ab_lambda": jnp.log(a0) - jnp.log1p(-a0),
        "ab_w_out": nrm((N_AB_LAYERS, D_MODEL, D_MODEL), D_MODEL ** -0.5),
        "pool_w": nrm((N_POOL_LAYERS, POOL_GROUPS, POOL_GD, POOL_GD), POOL_GD ** -0.5),
        "pool_scale": 1.0 + nrm((N_POOL_LAYERS, D_MODEL), 0.1),
        "ffn_w_up": nrm((DEPTH, D_MODEL, 2 * D_FF), D_MODEL ** -0.5),
        "ffn_conv_w": nrm((DEPTH, FFN_CONV, 2 * D_FF), FFN_CONV ** -0.5),
        "ffn_conv_b": nrm((DEPTH, 2 * D_FF), 0.02),
        "ffn_w_down": nrm((DEPTH, D_FF, D_MODEL), D_FF ** -0.5),
    }


def reference(x_prompt, x_sample, cache_k, cache_v, cache_logf, state_rg_h, state_rg_conv, state_pool,
              state_ffn_conv, page_table, norm_mix, norm_ffn, ab_w_in, ab_b_f, ab_q_gain, ab_k_gain,
              ab_conv_w, ab_conv_b, ab_w_a, ab_b_a, ab_w_x, ab_b_x, ab_lambda, ab_w_out, pool_w, pool_scale,
              ffn_w_up, ffn_conv_w, ffn_conv_b, ffn_w_down):
    p = {
        "norm_mix": norm_mix, "norm_ffn": norm_ffn,
        "ab_w_in": ab_w_in, "ab_b_f": ab_b_f, "ab_q_gain": ab_q_gain, "ab_k_gain": ab_k_gain,
        "ab_conv_w": ab_conv_w, "ab_conv_b": ab_conv_b, "ab_w_a": ab_w_a, "ab_b_a": ab_b_a,
        "ab_w_x": ab_w_x, "ab_b_x": ab_b_x, "ab_lambda": ab_lambda, "ab_w_out": ab_w_out,
        "pool_w": pool_w, "pool_scale": pool_scale,
        "ffn_w_up": ffn_w_up, "ffn_conv_w": ffn_conv_w, "ffn_conv_b": ffn_conv_b, "ffn_w_down": ffn_w_down,
    }
    bsz, dt = x_prompt.shape[0], x_prompt.dtype
    z_h = jnp.zeros((N_AB_LAYERS, bsz, B_WIDTH), dt)
    z_c = jnp.zeros((N_AB_LAYERS, bsz, RG_CONV - 1, B_WIDTH), dt)
    z_pool = jnp.zeros((N_POOL_LAYERS, bsz, POOL_BUF, D_MODEL), dt)
    z_ffn = jnp.zeros((DEPTH, bsz, FFN_CONV - 1, 2 * D_FF), dt)
    y_prompt, (k_p, v_p, logf_p, rg_h_p, rg_conv_p, pool_p, ffn_conv_p) = _trunk(
        x_prompt, 0, p, None, z_h, z_c, z_pool, z_ffn)
    past_len = page_table.shape[1] * cache_k.shape[2]
    y_sample, (k_s, v_s, logf_s, rg_h_s, rg_conv_s, pool_s, ffn_conv_s) = _trunk(
        x_sample, past_len, p, (cache_k, cache_v, cache_logf, page_table),
        state_rg_h, state_rg_conv, state_pool, state_ffn_conv)
    return (y_prompt, y_sample, k_p, v_p, logf_p, rg_h_p, rg_conv_p, pool_p, ffn_conv_p,
            k_s, v_s, logf_s, rg_h_s, rg_conv_s, pool_s, ffn_conv_s)
```

```python
import numpy as np
import concourse.bass as bass
import concourse.mybir as mybir
from concourse.bass_utils import run_bass_kernel_spmd
from contextlib import ExitStack

F32 = mybir.dt.float32
BF16 = mybir.dt.bfloat16
I32 = mybir.dt.int32
AF = mybir.ActivationFunctionType
ALU = mybir.AluOpType
AX = mybir.AxisListType

EPS = 1e-6
POOL_WINDOWS = (2, 4, 8, 16)


class Tok:
    __slots__ = ("w", "rs", "name")

    def __init__(self, name=""):
        self.w = None
        self.rs = []
        self.name = name


class Node:
    __slots__ = ("eng", "fn", "waits", "needed", "sigval", "idx", "dkey", "dval")

    def __init__(self, eng, fn):
        self.eng = eng
        self.fn = fn
        self.waits = []
        self.needed = False
        self.sigval = None
        self.idx = None
        self.dkey = None
        self.dval = None


ENGS = ("pe", "act", "dve", "pool", "sp")


class Sched:
    def __init__(self):
        self.ops = {e: [] for e in ENGS}
        self.seen = {e: {} for e in ENGS}
        self.dcnt = {}

    def _deps(self, eng, reads, writes):
        deps = []
        for t in reads:
            if t.w is not None:
                deps.append(t.w)
        for t in writes:
            if t.w is not None:
                deps.append(t.w)
            deps.extend(t.rs)
        best = {}
        for d in deps:
            if d.dkey is not None:
                k = ("d", d.dkey)
                v = d.dval
            else:
                if d.eng == "pe" and eng == "pe":
                    continue
                k = ("e", d.eng)
                v = d.idx
            if k not in best or best[k][0] < v:
                best[k] = (v, d)
        out = []
        seen = self.seen[eng]
        for k, (v, d) in best.items():
            if seen.get(k, -1) >= v:
                continue
            seen[k] = v
            if d.dkey is None:
                d.needed = True
            out.append(d)
        return out

    def _commit(self, node, reads, writes):
        for t in reads:
            if len(t.rs) > 24:
                last = {}
                for r in t.rs:
                    last[(r.eng, r.dkey)] = r
                t.rs = list(last.values())
            t.rs.append(node)
        for t in writes:
            t.w = node
            t.rs = []

    limit = None
    count = 0

    def op(self, eng, fn, reads=(), writes=()):
        self.count += 1
        if self.limit is not None and self.count > self.limit:
            return Node(eng, fn)
        node = Node(eng, fn)
        node.waits = self._deps(eng, reads, writes)
        node.idx = len(self.ops[eng])
        self.ops[eng].append(node)
        self._commit(node, reads, writes)
        return node

    def _deps_noprune(self, eng, reads, writes):
        deps = []
        for t in reads:
            if t.w is not None:
                deps.append(t.w)
        for t in writes:
            if t.w is not None:
                deps.append(t.w)
            deps.extend(t.rs)
        best = {}
        for d in deps:
            if d.dkey is not None:
                k, v = ("d", d.dkey), d.dval
            else:
                k, v = ("e", d.eng), d.idx
            if k not in best or best[k][0] < v:
                best[k] = (v, d)
        out = []
        for k, (v, d) in best.items():
            if d.dkey is None:
                d.needed = True
            out.append(d)
        return out

    def dma(self, q, fn, key, reads=(), writes=(), nodeps=False, before=None):
        self.count += 1
        if self.limit is not None and self.count > self.limit:
            return Node(q, fn)
        node = Node(q, fn)
        if nodeps:
            node.waits = []
        elif before is not None:
            node.waits = self._deps_noprune(q, reads, writes)
        else:
            node.waits = self._deps(q, reads, writes)
        node.idx = len(self.ops[q])
        node.dkey = key
        self.dcnt[key] = self.dcnt.get(key, 0) + 16
        node.dval = self.dcnt[key]
        if before is not None:
            pos = len(self.ops[q]) - 1
            while pos >= 0 and self.ops[q][pos] is not before:
                pos -= 1
            assert pos >= 0
            self.ops[q].insert(pos, node)
        else:
            self.ops[q].append(node)
        self._commit(node, reads, writes)
        return node

    def fence(self):
        lasts = {e: (self.ops[e][-1] if self.ops[e] else None) for e in ENGS}
        for e in ENGS:
            for n in reversed(self.ops[e]):
                if n.fn is not None and n.dkey is None:
                    lasts[e] = n
                    break
            else:
                lasts[e] = None
        for e in ENGS:
            node = Node(e, None)
            seen = self.seen[e]
            for x in ENGS:
                d = lasts[x]
                if x == e or d is None:
                    continue
                k = ("e", x)
                if seen.get(k, -1) >= d.idx:
                    continue
                seen[k] = d.idx
                d.needed = True
                node.waits.append(d)
            for key, cnt in self.dcnt.items():
                k = ("d", key)
                if seen.get(k, -1) >= cnt:
                    continue
                seen[k] = cnt
                pd = Node("sp", None)
                pd.dkey = key
                pd.dval = cnt
                node.waits.append(pd)
            node.idx = len(self.ops[e])
            self.ops[e].append(node)

    def emit(self, nc, stack, final_keys=()):
        esem = {e: stack.enter_context(nc.semaphore("s_" + e)) for e in ENGS}
        dsem = {k: stack.enter_context(nc.semaphore("d_" + str(k))) for k in self.dcnt}
        for e in ENGS:
            c = 0
            for n in self.ops[e]:
                if n.needed and n.dkey is None:
                    c += 1
                    n.sigval = c
        block = stack.enter_context(nc.Block())

        def run(eng_name):
            def body(eng):
                for n in self.ops[eng_name]:
                    for d in n.waits:
                        if d.dkey is not None:
                            eng.wait_ge(dsem[d.dkey], d.dval)
                        else:
                            eng.wait_ge(esem[d.eng], d.sigval)
                    if n.fn is None:
                        continue
                    inst = n.fn(eng)
                    if n.dkey is not None:
                        inst.then_inc(dsem[n.dkey], 16)
                    elif n.needed:
                        inst.then_inc(esem[eng_name], 1)
                if eng_name == "sp":
                    for k in final_keys:
                        if k in self.dcnt:
                            eng.wait_ge(dsem[k], self.dcnt[k])
            return body

        block.tensor(run("pe"))
        block.scalar(run("act"))
        block.vector(run("dve"))
        block.gpsimd(run("pool"))
        block.sync(run("sp"))


class Cfg:
    def __init__(self, SEQ=4096, DEPTH=4, NPG=2560, NPAGES=16):
        self.SEQ = SEQ
        self.DEPTH = DEPTH
        self.NPG = NPG
        self.NPAGES = NPAGES
        self.NAB = (DEPTH + 1) // 2
        self.NPL = DEPTH // 2
        self.CH = 1024
        self.NPASS = SEQ // self.CH
        self.NS = 16
        self.DL = 8


class Pass:
    pass


def build(cfg):
    SEQ, DEPTH, NAB, NPL, CH, NPASS = cfg.SEQ, cfg.DEPTH, cfg.NAB, cfg.NPL, cfg.CH, cfg.NPASS
    NS, DL, NPAGES, NPG = cfg.NS, cfg.DL, cfg.NPAGES, cfg.NPG
    NTG = SEQ // 128
    nc = bass.Bass("TRN2", target_bir_lowering=False)
    S = Sched()
    import os as _os
    if _os.environ.get("DBG_LIMIT"):
        S.limit = int(_os.environ["DBG_LIMIT"])
    st = ExitStack()

    def din(name, shape, dt=F32):
        return nc.dram_tensor(name, list(shape), dt, kind="ExternalInput").ap()

    def dout(name, shape, dt=F32):
        return nc.dram_tensor(name, list(shape), dt, kind="ExternalOutput").ap()

    def dscr(name, shape, dt=BF16):
        return nc.dram_tensor(name, list(shape), dt, kind="Internal").ap()

    xp = din("xp", [SEQ, 1024])
    xs = din("xs", [128, 1024])
    cache_k = din("cache_k", [NAB, NPG * 128, 512])
    cache_v = din("cache_v", [NAB, NPG * 128, 512])
    cache_lf = din("cache_lf", [NAB, NPG * 128, 8])
    st_rg_h = din("st_rg_h", [NAB, NS * 4, 128])
    st_rg_conv = din("st_rg_conv", [NAB, NS * 12, 128])
    st_pool = din("st_pool", [max(NPL, 1), NS * 15 * 8, 128])
    st_ffn = din("st_ffn", [DEPTH, NS * 96, 128])
    page_table = din("page_table", [1, NS * NPAGES], I32)
    norm_mix = din("norm_mix", [DEPTH, 1024])
    norm_ffn = din("norm_ffn", [DEPTH, 1024])
    ab_w_in = din("ab_w_in", [NAB, 1024, 2568])
    ab_b_f = din("ab_b_f", [NAB, 8])
    ab_q_gain = din("ab_q_gain", [NAB, 64])
    ab_k_gain = din("ab_k_gain", [NAB, 64])
    ab_par = din("ab_par", [NAB, 32, 128])
    ab_w_a = din("ab_w_a", [NAB, 8, 64, 64])
    ab_w_x = din("ab_w_x", [NAB, 8, 64, 64])
    ab_w_out = din("ab_w_out", [NAB, 1024, 1024])
    pool_w = din("pool_w", [max(NPL, 1), 4, 256, 256])
    pool_scale = din("pool_scale", [max(NPL, 1), 1024])
    ffn_w_up = din("ffn_w_up", [DEPTH, 1024, 6144])
    ffn_par = din("ffn_par", [DEPTH, 192, 128])
    ffn_w_down = din("ffn_w_down", [DEPTH, 3072, 1024])

    y_p = dout("y_p", [SEQ, 1024])
    y_s = dout("y_s", [128, 1024])
    k_p = dout("k_p", [NAB, SEQ, 512])
    v_p = dout("v_p", [NAB, SEQ, 512])
    logf_p = dout("logf_p", [NAB, SEQ, 8])
    rg_h_p = dout("rg_h_p", [NAB, 4, 128])
    rg_conv_p = dout("rg_conv_p", [NAB, 12, 128])
    pool_p = dout("pool_p", [max(NPL, 1), 15, 1024])
    ffn_conv_p = dout("ffn_conv_p", [DEPTH, 96, 128])
    k_s = dout("k_s", [NAB, 128, 512])
    v_s = dout("v_s", [NAB, 128, 512])
    logf_s = dout("logf_s", [NAB, 128, 8])
    rg_h_s = dout("rg_h_s", [NAB, NS * 4, 128])
    rg_conv_s = dout("rg_conv_s", [NAB, NS * 12, 128])
    pool_s = dout("pool_s", [max(NPL, 1), NS, 15, 1024])
    ffn_conv_s = dout("ffn_conv_s", [DEPTH, NS * 96, 128])

    WSCR_ELEMS = 128 * 4096 * (NAB * 12 + NPL * 2 + DEPTH * 20)
    w_scr = dscr("w_scr", [WSCR_ELEMS])
    kT_scr = dscr("kT_scr", [NAB, 4, 128, SEQ])
    V_scr = dscr("V_scr", [NAB, 128, NTG, 4, 192])

    def sb(name, shape, dt=F32):
        return st.enter_context(nc.sbuf_tensor(name, list(shape), dt))

    toks = {}

    def TK(name):
        if name not in toks:
            toks[name] = Tok(name)
        return toks[name]

    class Arena:
        def __init__(self, base2d, nbytes):
            self.base = base2d
            self.n = nbytes
            self.off = 0
            self.hi = 0

        def reset(self):
            self.off = 0

        def get(self, shape, dt=F32):
            esz = 4 if dt in (F32, I32) else 2
            cnt = int(np.prod(shape[1:]))
            nb = (cnt * esz + 3) // 4 * 4
            assert self.off + nb <= self.n, ("arena overflow", self.off, nb, self.n, shape)
            v = self.base[:, self.off // 4:(self.off + nb) // 4]
            self.off += nb
            self.hi = max(self.hi, self.off)
            if dt == BF16:
                v = v.bitcast(BF16)[:, 0:cnt]
            elif dt == I32:
                v = v.bitcast(I32)
            if len(shape) == 3:
                v = v.rearrange("p (a b) -> p a b", b=shape[2])
            elif len(shape) == 4:
                v = v.rearrange("p (a b c) -> p a b c", b=shape[2], c=shape[3])
            return v

    x = sb("x", [128, 8, 1024])
    xnT = sb("xnT", [128, 8, 1024], BF16)
    mT = sb("mT", [128, 8, 1024], BF16)
    arA_t = sb("arA", [128, 49152 // 4])
    arB_t = sb("arB", [128, 38400 // 4])
    arA = Arena(arA_t[:, :], 49152)
    arB = Arena(arB_t[:, :], 38400)
    arS = Arena(x[:, 1:8, :].rearrange("p a b -> p (a b)"), 7 * 4096)
    wsl = [sb(f"wsl{i}", [128, 4096], BF16) for i in range(3)]
    wfb = sb("wfb", [128, 8, 8], BF16)
    gsl = [sb("gsl0", [128, 1024])]
    xnb = [sb(f"xnb{i}", [128, 1024], BF16) for i in range(2)]
    ss = sb("ss", [128, 8])
    rs = sb("rs", [128, 8])
    identb = sb("identb", [128, 128], BF16)
    identf = sb("identf", [128, 128])
    utri = sb("utri", [128, 128])
    onesf = sb("onesf", [128, 128])
    buf_ = sb("buf_", [128, 128])
    maskc = sb("maskc", [128, 128], BF16)
    maskb = sb("maskb", [128, 128], BF16)
    onespad = sb("onespad", [128, 192], BF16)
    ones16 = sb("ones16", [128, 16])
    invcnt = sb("invcnt", [128, 16])
    lfall = [sb(f"lfall{i}", [128, NTG, 8]) for i in range(NAB)]
    ckall = [sb(f"ckall{i}", [128, NTG, 8]) for i in range(NAB)]
    cendall = [sb(f"cendall{i}", [128, NTG + 1, 8]) for i in range(NAB)]
    biasall = sb("biasall", [128, 2, NTG, 8])
    crefB = sb("crefB", [128, 2, 8])
    gq = sb("gq", [128, 64])
    gk = sb("gk", [128, 64])
    bfb = sb("bfb", [128, 8])
    gmx = sb("gmx", [128, 2])
    negB = sb("negB", [128, 1])
    apar = [sb(f"apar{i}", [128, 32]) for i in range(NAB)]
    spn = sb("spn", [128, 8])
    wabd = sb("wabd", [128, 4, 128], BF16)
    wxbd = sb("wxbd", [128, 4, 128], BF16)
    rgtail = [sb(f"rgtail{i}", [128, 12]) for i in range(NAB)]
    rghst = [sb(f"rghst{i}", [128, 4]) for i in range(NAB)]
    ptail = [sb(f"ptail{i}", [128, 120]) for i in range(max(NPL, 1))]
    fpar = [sb(f"fpar{i}", [128, 192]) for i in range(DEPTH)]
    ftail = [sb(f"ftail{i}", [128, 96]) for i in range(DEPTH)]
    ptab = sb("ptab", [128, NS * NPAGES], I32)
    idxall = ptab
    rowbuf = sb("rowbuf", [128, 4, 128])
    rowo = [sb(f"rowo{i}", [128, 128]) for i in range(2)]
    hT = arA.get([128, 24, 1024], BF16)
    arA.reset()
    qT = arA.get([128, 4, 1024], BF16)
    kTn = arA.get([128, 4, 1024], BF16)
    Vn = arA.get([128, 8, 4, 192], BF16)
    kcache = [arA.get([128, max(SEQ - CH, 128)], BF16)] * 2
    vcache = [arA.get([128, max(NTG - 8, 1), 192], BF16)] * 2
    NPT = 6
    ptb = [arB.get([128, 512], BF16) for i in range(NPT)]
    sq = arB.get([128, 512])
    tmpf = arB.get([128, 512])
    ssq = arB.get([128, 16])
    rq = arB.get([128, 16])
    qb = arB.get([128, 512], BF16)
    kb = arB.get([128, 512], BF16)
    kf = [arB.get([128, 512]) for i in range(2)]
    vf = [arB.get([128, 512]) for i in range(2)]
    t8 = arB.get([128, 8])
    e8 = arB.get([128, 8])
    rden = arB.get([128, 512])
    rge = arB.get([128, 516])
    rgacc = arB.get([128, 512])
    rgxb = arB.get([128, 512], BF16)
    rga = arB.get([128, 512])
    rgi = arB.get([128, 512])
    rgr = arB.get([128, 512])
    rgt = arB.get([128, 512])
    rgg = arB.get([128, 512])
    arB.reset()
    pe0 = arB.get([128, 528])
    peA = arB.get([128, 528])
    peB = arB.get([128, 528])
    ptmp = arB.get([128, 512])
    xnf = arB.get([128, 1024])
    arB.reset()
    NFS = 3
    fue = [[arB.get([128, 516]) for i in range(2)] for k in range(NFS)]
    facc = [[arB.get([128, 512]) for i in range(2)] for k in range(NFS)]
    fgl = [arB.get([128, 512]) for k in range(NFS)]
    arB.reset()
    pidx = arB.get([128, NS * NPAGES])
    ptf = arB.get([128, NS * NPAGES])
    rgtail_s = arS.get([128, NS * 12])
    rghst_s = arS.get([128, NS * 4])
    ptail_s = arS.get([128, NS * 120])
    ftail_s = arS.get([128, NS * 96])
    lfp = arS.get([128, NPAGES, 8])
    totA = arS.get([128, NPAGES, 8])
    incl = arS.get([128, NPAGES, 8])
    biasp = arS.get([128, NPAGES, 8])
    biasn = arS.get([128, 8])
    tb1 = arS.get([128, NPAGES, 8])
    kraw = [arS.get([128, 512], BF16) for i in range(2)]
    vraw = [arS.get([128, 512], BF16) for i in range(2)]
    kTp = [arS.get([128, 4, 128], BF16) for i in range(2)]
    vpad = [arS.get([128, 4, 192], BF16) for i in range(2)]
    sbs = arS.get([128, 64])
    pts = [arS.get([128, 64], BF16) for i in range(2)]
    rgtail_so, rghst_so, ftail_so = rgtail_s, rghst_s, ftail_s

    banks = [st.enter_context(nc.psum_tensor(f"ps{i}", [128, 512], F32)) for i in range(8)]
    btok = [TK(f"bank{i}") for i in range(8)]
    rot = {"all": 0, "lo": 0}

    def ps_next(pool="all"):
        if pool == "all":
            i = rot["all"] % 8
            rot["all"] += 1
        else:
            i = rot["lo"] % 4
            rot["lo"] += 1
        return banks[i], btok[i]

    def OP(eng, fn, reads=(), writes=()):
        return S.op(eng, fn, [r for r in reads if r is not None], [w for w in writes if w is not None])

    def DMA(q, out, in_, key, reads=(), writes=(), nodeps=False, before=None):
        return S.dma(q, lambda e: e.dma_start(out=out, in_=in_), key, list(reads), list(writes), nodeps=nodeps, before=before)

    out_keys = set()
    wmark = {}
    wtok = [TK(f"wsl{i}") for i in range(3)]
    wtok_ids = {id(t): i for i, t in enumerate(wtok)}

    def OUT(dst, src, srcname=None):
        key = "o_" + (srcname if srcname is not None else "misc")
        out_keys.add(key)
        return S.dma("sp", lambda e: e.dma_start(out=dst, in_=src), key, [TK(srcname)] if srcname is not None else [], [])

    def MM(out_ap, pairs, reads, writes, start=True, stop=True):
        def fn(e):
            inst = None
            n = len(pairs)
            for i, (l, r) in enumerate(pairs):
                inst = e.matmul(out_ap, lhsT=l, rhs=r, start=(start and i == 0), stop=(stop and i == n - 1),
                                skip_group_check=True)
            return inst
        nd = OP("pe", fn, reads, writes)
        for r in reads:
            i = wtok_ids.get(id(r))
            if i is not None:
                wmark[i] = (S.op("pool", None), S.op("sp", None))
        return nd

    wrot = [0]
    wseq = [0]
    wpass = [0]
    woff = [0]
    woffs = {}

    def wload(src, reads=()):
        i = wrot[0] % 3
        wrot[0] += 1
        shp = list(src.shape)
        n = int(np.prod(shp[1:]))
        assert n <= 4096 and shp[0] == 128, shp
        dst = wsl[i][:, 0:n]
        if len(shp) == 3:
            dst = dst.rearrange("p (a b) -> p a b", b=shp[2])
        elif len(shp) == 4:
            dst = dst.rearrange("p (a b c) -> p a b c", b=shp[2], c=shp[3])
        mk = wmark.get(i)
        widx = wseq[0]
        wseq[0] += 1
        flat = wsl[i][:, 0:n]
        if wpass[0] == 0:
            before = mk[0] if mk is not None else None
            if len(shp) == 4:
                for j in range(shp[2]):
                    DMA("pool", dst[:, :, j, :], src[:, :, j, :], f"w{i}", reads=reads, writes=[wtok[i]], nodeps=(j > 0), before=before)
            else:
                DMA("pool", dst, src, f"w{i}", reads=reads, writes=[wtok[i]], before=before)
            off = woff[0]
            woffs[widx] = (off, n)
            woff[0] += 128 * n
            assert woff[0] <= WSCR_ELEMS
            DMA("sp", w_scr[off:off + 128 * n].rearrange("(p n) -> p n", p=128), flat, f"wscr{i}", reads=[wtok[i]], writes=[TK(f"wscr_{widx}")])
        else:
            off, n0 = woffs[widx]
            assert n0 == n
            before = mk[1] if mk is not None else None
            DMA("sp", flat, w_scr[off:off + 128 * n].rearrange("(p n) -> p n", p=128), f"w{i}", reads=[TK(f"wscr_{widx}")] + list(reads),
                writes=[wtok[i]], before=before)
        return dst, wtok[i]

    gtok = [TK("gsl0")]

    def gload(row):
        i = 0
        DMA("sp", gsl[i][:, :], row.partition_broadcast(128), f"g{i}", writes=[gtok[i]])
        return gsl[i], gtok[i]

    def load_rows_T(dst, dtok, src, R):
        nt = (R + 127) // 128
        for t0 in range(0, nt, 4):
            t1 = min(nt, t0 + 4)
            r_end = min(R, t1 * 128)
            full = (r_end - t0 * 128) // 128
            if full:
                DMA("sp", rowbuf[:, 0:full, :], src[t0 * 128:(t0 + full) * 128, :].rearrange("(i p) c -> p i c", p=128), "rowbuf",
                    writes=[TK("rowbuf")])
            rem = (r_end - t0 * 128) % 128
            if rem:
                DMA("sp", rowbuf[0:rem, full, :], src[(t0 + full) * 128:r_end, :], "rowbuf", reads=[], writes=[TK("rowbuf")])
            bk, bt = ps_next()
            cols = 0
            for t in range(t0, t1):
                r = min(128, R - t * 128)
                OP("pe", lambda e, bk=bk, t=t, r=r, o=(t - t0) * 128, ti=t - t0: e.transpose(bk[:, o:o + r], rowbuf[0:r, ti, :], identf[0:r, 0:r]),
                   [TK("rowbuf"), TK("identf")], [bt])
                cols += r
            OP("act", lambda e, bk=bk, cols=cols, t0=t0: e.activation(out=dst[:, t0 * 128:t0 * 128 + cols], in_=bk[:, 0:cols], func=AF.Copy),
               [bt], [dtok])

    rorot = [0]

    def store_rows_T(src, stok, dstd, R, key="out"):
        nt = (R + 127) // 128
        for t in range(nt):
            r = min(128, R - t * 128)
            bk, bt = ps_next()
            OP("pe", lambda e, bk=bk, t=t, r=r: e.transpose(bk[0:r, 0:128], src[:, t * 128:t * 128 + r], identf[:, :]),
               [stok, TK("identf")], [bt])
            i = rorot[0] % 2
            rorot[0] += 1
            OP("act", lambda e, bk=bk, r=r, i=i: e.activation(out=rowo[i][0:r, :], in_=bk[0:r, 0:128], func=AF.Copy),
               [bt], [TK(f"rowo{i}")])
            OUT(dstd[t * 128:t * 128 + r, :], rowo[i][0:r, :], f"rowo{i}")

    def consts():
        for t_, nm in ((identb, "identb"), (identf, "identf"), (utri, "utri"), (buf_, "buf_"), (onesf, "onesf"),
                       (ones16, "ones16")):
            OP("pool", lambda e, t_=t_: e.memset(t_[:], 1.0), [], [TK(nm)])
        for t_, nm in ((identb, "identb"), (identf, "identf")):
            OP("pool", lambda e, t_=t_: e.affine_select(out=t_[:], in_=t_[:], pattern=[[-1, 128]], compare_op=ALU.is_equal,
                                                          fill=0.0, base=0, channel_multiplier=1), [TK(nm)], [TK(nm)])
        for t_, nm in ((utri, "utri"), (buf_, "buf_")):
            OP("pool", lambda e, t_=t_: e.affine_select(out=t_[:], in_=t_[:], pattern=[[1, 128]], compare_op=ALU.is_ge,
                                                          fill=0.0, base=0, channel_multiplier=-1), [TK(nm)], [TK(nm)])
        b3 = buf_[:].rearrange("p (b t) -> p b t", t=8)
        OP("pool", lambda e: e.affine_select(out=b3, in_=b3, pattern=[[-8, 16], [0, 8]], compare_op=ALU.is_ge,
                                              fill=0.0, base=0, channel_multiplier=1), [TK("buf_")], [TK("buf_")])
        OP("pool", lambda e: e.tensor_copy(out=maskc[:], in_=utri[:]), [TK("utri")], [TK("maskc")])
        OP("pool", lambda e: e.tensor_copy(out=maskb[:], in_=buf_[:]), [TK("buf_")], [TK("maskb")])
        OP("pool", lambda e: e.memset(onespad[:], 1.0), [], [TK("onespad")])
        OP("pool", lambda e: e.memset(onespad[:, 64:128], 0.0), [TK("onespad")], [TK("onespad")])
        OP("pool", lambda e: e.iota(invcnt[:], pattern=[[1, 16]], base=1, channel_multiplier=0,
                                    allow_small_or_imprecise_dtypes=True), [], [TK("invcnt")])
        OP("dve", lambda e: e.reciprocal(out=invcnt[:], in_=invcnt[:]), [TK("invcnt")], [TK("invcnt")])
        for l in range(DEPTH):
            load_rows_T(fpar[l], TK(f"fpar{l}"), ffn_par[l], 192)
        for li in range(NAB):
            load_rows_T(apar[li], TK(f"apar{li}"), ab_par[li], 32)
        DMA("sp", ptab[:, :], page_table.partition_broadcast(128), "ptab", writes=[TK("ptab")])
        OP("pool", lambda e: e.iota(pidx[:], pattern=[[0, NS * NPAGES]], base=0, channel_multiplier=1,
                                    allow_small_or_imprecise_dtypes=True), [], [TK("pidx")])
        OP("dve", lambda e: e.tensor_copy(out=ptf[:], in_=ptab[:]), [TK("ptab")], [TK("ptf")])
        OP("dve", lambda e: e.scalar_tensor_tensor(out=ptf[:], in0=ptf[:], scalar=128.0, in1=pidx[:], op0=ALU.mult, op1=ALU.add),
           [TK("ptf"), TK("pidx")], [TK("ptf")])
        OP("dve", lambda e: e.tensor_copy(out=idxall[:], in_=ptf[:]), [TK("ptf"), TK("ptab")], [TK("idxall")])

    def cview(ap2, ps_, col0, n):
        v = ap2[:, col0:col0 + n]
        if ps_.sample:
            v = v.rearrange("p (s t) -> p s t", t=DL)
        return v

    def eview(ext, ps_, Hh, col0, n, shift):
        if ps_.sample:
            e3 = ext[:, 0:NS * (Hh + DL)].rearrange("p (s t) -> p s t", t=Hh + DL)
            return e3[:, :, Hh + shift:Hh + shift + DL]
        return ext[:, Hh + col0 + shift:Hh + col0 + shift + n]

    def norm(ps_, grow, want_f32_last=False):
        g, gt_ = gload(grow)
        NT = ps_.NT
        for tt in range(NT):
            OP("act", lambda e, tt=tt: e.activation(out=xnb[tt % 2][:, :], in_=x[:, tt, :], func=AF.Square,
                                                      accum_out=ss[:, tt:tt + 1]),
               [TK(f"x{tt}"), TK("rs")], [TK(f"ss{tt}"), TK(f"xnb{tt % 2}")])
        OP("act", lambda e: e.activation(out=rs[:, 0:NT], in_=ss[:, 0:NT], func=AF.Sqrt, scale=1.0 / 1024, bias=EPS),
           [TK(f"ss{t}") for t in range(NT)], [TK("rs")])
        OP("dve", lambda e: e.reciprocal(out=rs[:, 0:NT], in_=rs[:, 0:NT]), [TK("rs")], [TK("rs")])
        for tt in range(NT):
            xb_ = xnb[tt % 2]
            xt_ = TK(f"xnb{tt % 2}")
            OP("dve", lambda e, tt=tt, xb_=xb_: e.scalar_tensor_tensor(out=xb_[:, :], in0=x[:, tt, :], scalar=rs[:, tt:tt + 1],
                                                                       in1=g[:, :], op0=ALU.mult, op1=ALU.mult),
               [TK(f"x{tt}"), TK("rs"), gt_], [xt_])
            if want_f32_last and tt == NT - 1:
                OP("dve", lambda e, tt=tt: e.scalar_tensor_tensor(out=xnf[:, :], in0=x[:, tt, :], scalar=rs[:, tt:tt + 1],
                                                                  in1=g[:, :], op0=ALU.mult, op1=ALU.mult),
                   [TK(f"x{tt}"), TK("rs"), gt_], [TK("xnf")])
            bk, bt = ps_next()
            pb = bk[:].bitcast(BF16)

            def tr(e, pb=pb, xb_=xb_):
                inst = None
                for kc in range(8):
                    inst = e.transpose(pb[:, kc * 128:(kc + 1) * 128], xb_[:, kc * 128:(kc + 1) * 128], identb[:])
                return inst
            OP("pe", tr, [xt_, TK("identb")], [bt])
            OP("act", lambda e, pb=pb, tt=tt: e.activation(out=xnT[:, :, tt * 128:(tt + 1) * 128],
                                                            in_=pb.rearrange("p (k t) -> p k t", t=128), func=AF.Copy),
               [bt], [TK(f"xnT{tt}")])

    def xn_tiles(ps_, col0, n):
        return [TK(f"xnT{t}") for t in range(col0 // 128, (col0 + n + 127) // 128)]

    def m_tiles(col0, n):
        return [TK(f"mT{t}") for t in range(col0 // 128, (col0 + n + 127) // 128)]

    def ab_setup(li):
        DMA("sp", gq[:, :], ab_q_gain[li:li + 1, :].partition_broadcast(128), "gq", writes=[TK("gq")])
        DMA("sp", gk[:, :], ab_k_gain[li:li + 1, :].partition_broadcast(128), "gk", writes=[TK("gk")])
        DMA("sp", bfb[:, :], ab_b_f[li:li + 1, :].partition_broadcast(128), "bfb", writes=[TK("bfb")])
        OP("dve", lambda e: e.tensor_reduce(out=gmx[:, 0:1], in_=gq[:, :], axis=AX.X, op=ALU.max, apply_absolute_value=True),
           [TK("gq")], [TK("gmx")])
        OP("dve", lambda e: e.tensor_reduce(out=gmx[:, 1:2], in_=gk[:, :], axis=AX.X, op=ALU.max, apply_absolute_value=True),
           [TK("gk"), TK("gmx")], [TK("gmx")])
        OP("dve", lambda e: e.scalar_tensor_tensor(out=negB[:, :], in0=gmx[:, 0:1], scalar=-8.0, in1=gmx[:, 1:2],
                                                   op0=ALU.mult, op1=ALU.mult), [TK("gmx")], [TK("negB")])
        ap_ = apar[li]
        OP("act", lambda e: e.activation(out=spn[:, 0:4], in_=ap_[:, 28:32], func=AF.Exp, scale=-1.0),
           [TK(f"apar{li}")], [TK("spn")])
        OP("act", lambda e: e.activation(out=spn[:, 0:4], in_=spn[:, 0:4], func=AF.Ln, bias=1.0), [TK("spn")], [TK("spn")])
        OP("dve", lambda e: e.tensor_scalar(out=spn[:, 4:8], in0=spn[:, 0:4], scalar1=-16.0, scalar2=None, op0=ALU.mult),
           [TK("spn")], [TK("spn")])
        OP("dve", lambda e: e.tensor_scalar(out=spn[:, 0:4], in0=spn[:, 0:4], scalar1=-8.0, scalar2=None, op0=ALU.mult),
           [TK("spn")], [TK("spn")])
        OP("pool", lambda e: e.memset(wabd[:], 0.0), [], [TK("wabd")])
        OP("pool", lambda e: e.memset(wxbd[:], 0.0), [], [TK("wxbd")])
        for hh in range(2):
            DMA("pool", wabd[hh * 64:(hh + 1) * 64, :, hh * 64:(hh + 1) * 64],
                ab_w_a[li].rearrange("(c two) i j -> two i c j", two=2)[hh], "wabd", reads=[], writes=[TK("wabd")])
            DMA("pool", wxbd[hh * 64:(hh + 1) * 64, :, hh * 64:(hh + 1) * 64],
                ab_w_x[li].rearrange("(c two) i j -> two i c j", two=2)[hh], "wxbd", reads=[], writes=[TK("wxbd")])

    def ab_tokmajor(ps_, li):
        Wv = ab_w_in[li].rearrange("(kc p) n -> p kc n", p=128)
        wq, wqt = wload(Wv[:, :, 0:512])
        wk, wkt = wload(Wv[:, :, 512:1024])
        wv, wvt = wload(Wv[:, :, 1024:1536])
        DMA("pool", wfb[:, :, :], Wv[:, :, 1536:1544], "wfb", writes=[TK("wfb")])
        wf, wft = wfb, TK("wfb")
        OP("pool", lambda e: e.memset(Vn[:, :, :, 64:128], 0.0), [TK("Vn")], [TK("Vn")])
        lfa, cka, cea = lfall[li], ckall[li], cendall[li]
        tl, tc, te = TK(f"lfall{li}"), TK(f"ckall{li}"), TK(f"cendall{li}")
        if ps_.sample or ps_.chunk == 0:
            OP("dve", lambda e: e.memset(cea[:, 0, :], 0.0), [], [te])
        for tt in range(ps_.NT):
            gt = ps_.gt0 + tt
            xt = [TK(f"xnT{tt}")]
            cols = slice(tt * 128, (tt + 1) * 128)
            pq, pqt = ps_next()
            pk, pkt = ps_next()
            pv, pvt = ps_next()
            pf, pft = ps_next()
            MM(pq[:, :], [(xnT[:, kc, cols], wq[:, kc, :]) for kc in range(8)], xt + [wqt], [pqt])
            MM(pk[:, :], [(xnT[:, kc, cols], wk[:, kc, :]) for kc in range(8)], xt + [wkt], [pkt])
            MM(pv[:, :], [(xnT[:, kc, cols], wv[:, kc, :]) for kc in range(8)], xt + [wvt], [pvt])
            MM(pf[:, 0:8], [(xnT[:, kc, cols], wf[:, kc, :]) for kc in range(8)], xt + [wft], [pft])
            for which, (pp, ppt) in enumerate(((pq, pqt), (pk, pkt))):
                so = which * 8
                OP("act", lambda e, pp=pp: e.activation(out=sq[:, :], in_=pp[:, :], func=AF.Square), [ppt], [TK("sq")])
                OP("dve", lambda e, so=so: e.tensor_reduce(out=ssq[:, so:so + 8], in_=sq[:, :].rearrange("p (h d) -> p h d", d=64),
                                                           axis=AX.X, op=ALU.add), [TK("sq")], [TK("ssq")])
                OP("act", lambda e, so=so: e.activation(out=rq[:, so:so + 8], in_=ssq[:, so:so + 8], func=AF.Sqrt,
                                                        scale=1.0 / 64, bias=EPS), [TK("ssq")], [TK("rq")])
                OP("dve", lambda e, so=so: e.reciprocal(out=rq[:, so:so + 8], in_=rq[:, so:so + 8]), [TK("rq")], [TK("rq")])
                OP("dve", lambda e, pp=pp, so=so: e.tensor_tensor(out=tmpf[:, :].rearrange("p (h d) -> p h d", d=64),
                                                                  in0=pp[:, :].rearrange("p (h d) -> p h d", d=64),
                                                                  in1=rq[:, so:so + 8].unsqueeze(2).broadcast_to([128, 8, 64]),
                                                                  op=ALU.mult), [ppt, TK("rq")], [TK("tmpf")])
                if which == 0:
                    OP("dve", lambda e: e.tensor_tensor(out=qb[:, :].rearrange("p (h d) -> p h d", d=64),
                                                        in0=tmpf[:, :].rearrange("p (h d) -> p h d", d=64),
                                                        in1=gq[:, :].unsqueeze(1).broadcast_to([128, 8, 64]), op=ALU.mult),
                       [TK("tmpf"), TK("gq")], [TK("qb")])
                else:
                    kf_ = kf[tt % 2]
                    kft = TK(f"kf{tt % 2}")
                    OP("dve", lambda e, kf_=kf_: e.tensor_tensor(out=kf_[:, :].rearrange("p (h d) -> p h d", d=64),
                                                                 in0=tmpf[:, :].rearrange("p (h d) -> p h d", d=64),
                                                                 in1=gk[:, :].unsqueeze(1).broadcast_to([128, 8, 64]), op=ALU.mult),
                       [TK("tmpf"), TK("gk")], [kft])
                    OP("act", lambda e, kf_=kf_: e.activation(out=kb[:, :], in_=kf_[:, :], func=AF.Copy), [kft], [TK("kb")])
                    OUT(ps_.k_out(li)[tt * 128:(tt + 1) * 128, :], kf_[:, :], f"kf{tt % 2}")
            vf_ = vf[tt % 2]
            vft = TK(f"vf{tt % 2}")
            OP("act", lambda e, vf_=vf_, pv=pv: e.activation(out=vf_[:, :], in_=pv[:, :], func=AF.Copy), [pvt], [vft])
            OUT(ps_.v_out(li)[tt * 128:(tt + 1) * 128, :], vf_[:, :], f"vf{tt % 2}")
            v4 = vf_[:, :].rearrange("p (c two d) -> p c two d", two=2, d=64)
            OP("pool", lambda e, tt=tt, v4=v4: e.tensor_copy(out=Vn[:, tt, :, 0:64], in_=v4[:, :, 0, :]), [vft], [TK("Vn")])
            OP("pool", lambda e, tt=tt, v4=v4: e.tensor_copy(out=Vn[:, tt, :, 128:192], in_=v4[:, :, 1, :]), [vft], [TK("Vn")])
            OP("dve", lambda e, pf=pf: e.tensor_tensor(out=t8[:, :], in0=pf[:, 0:8], in1=bfb[:, :], op=ALU.add),
               [pft, TK("bfb")], [TK("t8")])
            OP("act", lambda e: e.activation(out=e8[:, :], in_=t8[:, :], func=AF.Exp, scale=-1.0), [TK("t8")], [TK("e8")])
            OP("act", lambda e: e.activation(out=e8[:, :], in_=e8[:, :], func=AF.Ln, bias=1.0), [TK("e8")], [TK("e8")])
            OP("dve", lambda e, gt=gt: e.tensor_scalar(out=lfa[:, gt, :], in0=e8[:, :], scalar1=-1.0, scalar2=None, op0=ALU.mult),
               [TK("e8")], [tl])
            OUT(ps_.lf_out(li)[tt * 128:(tt + 1) * 128, :], lfa[:, gt, :], f"lfall{li}")
            pc, pct = ps_next()
            tri = buf_ if ps_.sample else utri
            MM(pc[:, 0:8], [(tri[:, :], lfa[:, gt, :])], [tl, TK("utri"), TK("buf_")], [pct])
            if ps_.sample:
                OP("dve", lambda e, pc=pc: e.scalar_tensor_tensor(out=biasn[:, :], in0=pc[:, 0:8], scalar=-1.0,
                                                                  in1=negB[:, 0:1].broadcast_to([128, 8]),
                                                                  op0=ALU.mult, op1=ALU.add), [pct, TK("negB")], [TK("biasn")])
            else:
                MM(pc[:, 8:16], [(onesf[:, :], lfa[:, gt, :])], [tl, TK("onesf")], [pct])
                OP("dve", lambda e, pc=pc, gt=gt: e.tensor_tensor(out=cka[:, gt, :], in0=pc[:, 0:8], in1=cea[:, gt, :], op=ALU.add),
                   [pct, te], [tc])
                OP("dve", lambda e, pc=pc, gt=gt: e.tensor_tensor(out=cea[:, gt + 1, :], in0=pc[:, 8:16], in1=cea[:, gt, :], op=ALU.add),
                   [pct, te], [te])
            for src, stok, dstT, dname in ((qb, TK("qb"), qT, "qT"), (kb, TK("kb"), kTn, "kTn")):
                bk, bt = ps_next()
                pb = bk[:].bitcast(BF16)

                def tr(e, pb=pb, src=src):
                    inst = None
                    for c in range(4):
                        inst = e.transpose(pb[:, c * 128:(c + 1) * 128], src[:, c * 128:(c + 1) * 128], identb[:])
                    return inst
                OP("pe", tr, [stok, TK("identb")], [bt])
                OP("act", lambda e, pb=pb, dstT=dstT, cols=cols: e.activation(out=dstT[:, :, cols],
                                                                              in_=pb[:, 0:512].rearrange("p (c t) -> p c t", t=128),
                                                                              func=AF.Copy), [bt], [TK(f"{dname}{tt}")])
        if (not ps_.sample) and ps_.chunk < NPASS - 1:
            i = ps_.chunk
            DMA("sp", kT_scr[li].rearrange("c p t -> p c t")[:, :, i * CH:(i + 1) * CH], kTn[:, :, :], f"kscr{li}",
                reads=[TK(f"kTn{t}") for t in range(8)], writes=[TK(f"kscr{li}")])
            DMA("sp", V_scr[li][:, i * 8:(i + 1) * 8], Vn[:, :, :, :], f"vscr{li}", reads=[TK("Vn")], writes=[TK(f"vscr{li}")])

    def ab_rg(ps_, li):
        Wv = ab_w_in[li].rearrange("(kc p) n -> p kc n", p=128)
        wxr, wxrt = wload(Wv[:, :, 1544:2056])
        wgt, wgtt = wload(Wv[:, :, 2056:2568])
        ap_ = apar[li]
        apt = TK(f"apar{li}")
        smp = ps_.sample
        for c in range(4):
            for (col0, n) in ps_.blocks:
                xt = xn_tiles(ps_, col0, n)
                if smp:
                    hv = rgtail_s[:, :].rearrange("p (s j c) -> p s j c", j=3, c=4)[:, :, :, c]
                    e3 = rge[:, 0:NS * 11].rearrange("p (s t) -> p s t", t=11)
                    OP("pool", lambda e, hv=hv, e3=e3: e.tensor_copy(out=e3[:, :, 0:3], in_=hv), [TK("rgtail_s")], [TK("rge")])
                else:
                    hv = rgtail[li][:, :].rearrange("p (j c) -> p j c", c=4)[:, :, c]
                    OP("pool", lambda e, hv=hv: e.tensor_copy(out=rge[:, 0:3], in_=hv), [TK(f"rgtail{li}")], [TK("rge")])
                px, pxt = ps_next()
                MM(px[:, 0:n], [(wxr[:, kc, c * 128:(c + 1) * 128], xnT[:, kc, col0:col0 + n]) for kc in range(8)],
                   xt + [wxrt], [pxt])
                pxv = px[:, 0:n].rearrange("p (s t) -> p s t", t=DL) if smp else px[:, 0:n]
                OP("act", lambda e, pxv=pxv, n=n: e.activation(out=eview(rge, ps_, 3, 0, n, 0), in_=pxv, func=AF.Copy),
                   [pxt], [TK("rge")])
                OP("act", lambda e, n=n, c=c: e.activation(out=cview(rgacc, ps_, 0, n), in_=eview(rge, ps_, 3, 0, n, 0),
                                                           func=AF.Identity, scale=ap_[:, 12 + c:13 + c], bias=ap_[:, 16 + c:17 + c]),
                   [TK("rge"), apt], [TK("rgacc")])
                for j in (2, 1, 0):
                    OP("dve", lambda e, n=n, c=c, j=j: e.scalar_tensor_tensor(
                        out=cview(rgacc, ps_, 0, n), in0=eview(rge, ps_, 3, 0, n, j - 3), scalar=ap_[:, j * 4 + c:j * 4 + c + 1],
                        in1=cview(rgacc, ps_, 0, n), op0=ALU.mult, op1=ALU.add), [TK("rge"), TK("rgacc"), apt], [TK("rgacc")])
                if smp:
                    to = rgtail_s[:, :].rearrange("p (s j c) -> p s j c", j=3, c=4)[:, :, :, c]
                    OP("pool", lambda e, e3=e3, to=to: e.tensor_copy(out=to, in_=e3[:, :, 8:11]), [TK("rge")], [TK("rgtail_s")])
                else:
                    OP("pool", lambda e, hv=hv, n=n: e.tensor_copy(out=hv, in_=rge[:, n:n + 3]), [TK("rge")], [TK(f"rgtail{li}")])
                OP("act", lambda e, n=n: e.activation(out=rgxb[:, 0:n], in_=rgacc[:, 0:n], func=AF.Copy), [TK("rgacc")], [TK("rgxb")])
                pr, prt = ps_next()
                pi, pit = ps_next()
                MM(pr[:, 0:n], [(wabd[:, c, :], rgxb[:, 0:n])], [TK("wabd"), TK("rgxb")], [prt])
                MM(pi[:, 0:n], [(wxbd[:, c, :], rgxb[:, 0:n])], [TK("wxbd"), TK("rgxb")], [pit])
                OP("act", lambda e, pr=pr, n=n, c=c: e.activation(out=rgr[:, 0:n], in_=pr[:, 0:n], func=AF.Sigmoid,
                                                                  bias=ap_[:, 20 + c:21 + c]), [prt, apt], [TK("rgr")])
                OP("act", lambda e, pi=pi, n=n, c=c: e.activation(out=rgi[:, 0:n], in_=pi[:, 0:n], func=AF.Sigmoid,
                                                                  bias=ap_[:, 24 + c:25 + c]), [pit, apt], [TK("rgi")])
                OP("act", lambda e, n=n, c=c: e.activation(out=rga[:, 0:n], in_=rgr[:, 0:n], func=AF.Exp, scale=spn[:, c:c + 1]),
                   [TK("rgr"), TK("spn")], [TK("rga")])
                OP("act", lambda e, n=n, c=c: e.activation(out=rgt[:, 0:n], in_=rgr[:, 0:n], func=AF.Exp, scale=spn[:, 4 + c:5 + c]),
                   [TK("rgr"), TK("spn")], [TK("rgt")])
                OP("act", lambda e, n=n: e.activation(out=rgt[:, 0:n], in_=rgt[:, 0:n], func=AF.Sqrt, scale=-1.0, bias=1.0),
                   [TK("rgt")], [TK("rgt")])
                OP("dve", lambda e, n=n: e.tensor_tensor(out=rgi[:, 0:n], in0=rgi[:, 0:n], in1=rgt[:, 0:n], op=ALU.mult),
                   [TK("rgi"), TK("rgt")], [TK("rgi")])
                OP("dve", lambda e, n=n: e.tensor_tensor(out=rgi[:, 0:n], in0=rgi[:, 0:n], in1=rgacc[:, 0:n], op=ALU.mult),
                   [TK("rgi"), TK("rgacc")], [TK("rgi")])
                if smp:
                    a3 = rga[:, 0:n].rearrange("p (s t) -> p s t", t=DL)
                    i3 = rgi[:, 0:n].rearrange("p (s t) -> p s t", t=DL)
                    hs = rghst_s[:, :].rearrange("p (s c) -> p s c", c=4)[:, :, c:c + 1]
                    OP("dve", lambda e, a3=a3, hs=hs: e.tensor_tensor(out=rgt[:, 0:NS].unsqueeze(2), in0=a3[:, :, 0:1], in1=hs, op=ALU.mult),
                       [TK("rga"), TK("rghst_s")], [TK("rgt")])
                    OP("dve", lambda e, i3=i3: e.tensor_tensor(out=i3[:, :, 0:1], in0=i3[:, :, 0:1], in1=rgt[:, 0:NS].unsqueeze(2), op=ALU.add),
                       [TK("rgi"), TK("rgt")], [TK("rgi")])
                    OP("dve", lambda e, a3=a3: e.memset(a3[:, :, 0:1], 0.0), [TK("rga")], [TK("rga")])
                    OP("dve", lambda e, n=n: e.tensor_tensor_scan(out=rgacc[:, 0:n], data0=rga[:, 0:n], data1=rgi[:, 0:n], initial=0.0,
                                                                  op0=ALU.mult, op1=ALU.add), [TK("rga"), TK("rgi")], [TK("rgacc")])
                    h3 = rgacc[:, 0:n].rearrange("p (s t) -> p s t", t=DL)
                    OP("pool", lambda e, h3=h3, hs=hs: e.tensor_copy(out=hs, in_=h3[:, :, DL - 1:DL]), [TK("rgacc")], [TK("rghst_s")])
                else:
                    OP("dve", lambda e, c=c, n=n: e.tensor_tensor_scan(out=rgacc[:, 0:n], data0=rga[:, 0:n], data1=rgi[:, 0:n],
                                                                       initial=rghst[li][:, c:c + 1], op0=ALU.mult, op1=ALU.add),
                       [TK("rga"), TK("rgi"), TK(f"rghst{li}")], [TK("rgacc")])
                    OP("pool", lambda e, c=c, n=n: e.tensor_copy(out=rghst[li][:, c:c + 1], in_=rgacc[:, n - 1:n]),
                       [TK("rgacc")], [TK(f"rghst{li}")])
                pg, pgt = ps_next()
                MM(pg[:, 0:n], [(wgt[:, kc, c * 128:(c + 1) * 128], xnT[:, kc, col0:col0 + n]) for kc in range(8)],
                   xt + [wgtt], [pgt])
                OP("act", lambda e, pg=pg, n=n: e.activation(out=rgg[:, 0:n], in_=pg[:, 0:n], func=AF.Gelu_apprx_tanh), [pgt], [TK("rgg")])
                OP("dve", lambda e, n=n, col0=col0, c=c: e.tensor_tensor(out=mT[:, 4 + c, col0:col0 + n], in0=rgacc[:, 0:n],
                                                                         in1=rgg[:, 0:n], op=ALU.mult),
                   [TK("rgacc"), TK("rgg")], m_tiles(col0, n))

    ND = [((banks[4], btok[4]), (banks[5], btok[5])), ((banks[6], btok[6]), (banks[7], btok[7]))]
    ptrot = [0]

    def attn_prompt(ps_, li):
        i = ps_.chunk
        cka, cea = ckall[li], cendall[li]
        tc, te = TK(f"ckall{li}"), TK(f"cendall{li}")
        for b in range(2):
            gmid = i * 8 + 4 * b + 2
            OP("dve", lambda e, b=b, gmid=gmid: e.tensor_tensor(out=crefB[:, b, :], in0=cea[:, gmid, :],
                                                                in1=negB[:, 0:1].broadcast_to([128, 8]), op=ALU.add),
               [te, TK("negB")], [TK("crefB")])
            ng = i * 8 + 4 * b + 4
            OP("dve", lambda e, b=b, ng=ng: e.scalar_tensor_tensor(out=biasall[:, b, 0:ng, :], in0=cka[:, 0:ng, :], scalar=-1.0,
                                                                   in1=crefB[:, b:b + 1, :].broadcast_to([128, ng, 8]),
                                                                   op0=ALU.mult, op1=ALU.add), [tc, TK("crefB")], [TK("biasall")])
        qts = [TK(f"qT{t}") for t in range(8)]
        LA = 3
        for c in range(4):
            kc_, vc_ = kcache[0], vcache[0]
            kct, vct = TK("kcache0"), TK("vcache0")
            if i > 0:
                DMA("sp", kc_[:, 0:i * CH], kT_scr[li][c][:, 0:i * CH], "kc0", reads=[TK(f"kscr{li}")], writes=[kct])
                DMA("sp", vc_[:, 0:i * 8, :], V_scr[li][:, 0:i * 8, c, :], "vc0", reads=[TK(f"vscr{li}")], writes=[vct])
            items = []
            for b in range(2):
                ktiles = [(True, g) for g in range(i * 8)] + [(False, t) for t in range(4 * b + 4)]
                nk = len(ktiles)
                for ki, (past, t) in enumerate(ktiles):
                    for hh in range(2):
                        items.append((b, ki, nk, past, t, hh))
            staged = {}

            def stageA(it, c=c):
                b, ki, nk, past, t, hh = it
                g = t if past else i * 8 + t
                diag = (not past) and t >= 4 * b
                qlo = (t - 4 * b) * 128 if diag else 0
                h = 2 * c + hh
                r0 = hh * 64
                pS, pSt = ps_next("lo")
                if past:
                    lhsT = kc_[r0:r0 + 64, t * 128:(t + 1) * 128]
                    lr = [kct]
                else:
                    lhsT = kTn[r0:r0 + 64, c, t * 128:(t + 1) * 128]
                    lr = [TK(f"kTn{t}")]
                MM(pS[:, qlo:512], [(lhsT, qT[r0:r0 + 64, c, b * 512 + qlo:b * 512 + 512])], lr + qts[4 * b:4 * b + 4], [pSt])
                pi_ = ptrot[0] % NPT
                ptrot[0] += 1
                PT, PTt = ptb[pi_], TK(f"ptb{pi_}")
                OP("act", lambda e, pS=pS, PT=PT, qlo=qlo, b=b, g=g, h=h: e.activation(
                    out=PT[:, qlo:512], in_=pS[:, qlo:512], func=AF.Exp, scale=0.125, bias=biasall[:, b, g, h:h + 1]),
                   [pSt, TK("biasall")], [PTt])
                if diag:
                    OP("pool", lambda e, PT=PT, qlo=qlo: e.tensor_tensor(out=PT[:, qlo:qlo + 128], in0=PT[:, qlo:qlo + 128],
                                                                          in1=maskc[:, :], op=ALU.mult), [PTt, TK("maskc")], [PTt])
                staged[it] = (PT, PTt, qlo)

            def stageB(it, c=c):
                b, ki, nk, past, t, hh = it
                PT, PTt, qlo = staged.pop(it)
                r0 = hh * 64
                (Nb, Nt), (Db, Dt) = ND[(c * 2 + b) % 2]
                if past:
                    vl = vc_[:, t, r0:r0 + 128]
                    vr = [vct]
                else:
                    vl = Vn[:, t, c, r0:r0 + 128]
                    vr = [TK("Vn")]
                first = (ki == 0 and hh == 0)
                last = (ki == nk - 1 and hh == 1)
                MM(Nb[:, qlo:512], [(vl, PT[:, qlo:512])], vr + [PTt], [Nt], start=first, stop=last)
                MM(Db[:, qlo:512], [(onespad[:, r0:r0 + 128], PT[:, qlo:512])], [TK("onespad"), PTt], [Dt], start=first, stop=last)
                if last:
                    OP("dve", lambda e, Db=Db: e.reciprocal(out=rden[:, :], in_=Db[:, :]), [Dt], [TK("rden")])
                    OP("dve", lambda e, Nb=Nb, c=c, b=b: e.tensor_tensor(out=mT[:, c, b * 512:(b + 1) * 512], in0=Nb[:, :], in1=rden[:, :],
                                                                         op=ALU.mult), [Nt, TK("rden")], m_tiles(b * 512, 512))
            n_it = len(items)
            for k in range(n_it + LA):
                if k < n_it:
                    stageA(items[k])
                if k >= LA:
                    stageB(items[k - LA])

    grot3 = [0]

    def attn_sample(ps_, li):
        (Nb, Nt), (Db, Dt) = ND[0]
        qts = [TK("qT0")]
        for i in range(2):
            OP("pool", lambda e, i=i: e.memset(vpad[i][:, :, 64:128], 0.0), [TK(f"vpad{i}")], [TK(f"vpad{i}")])
        for c in range(4):
            for hh in range(2):
                h = 2 * c + hh
                r0 = hh * 64
                pS, pSt = ps_next("lo")
                MM(pS[:, 0:128], [(kTn[r0:r0 + 64, c, 0:128], qT[r0:r0 + 64, c, 0:128])], [TK("kTn0")] + qts, [pSt])
                pi_ = ptrot[0] % NPT
                ptrot[0] += 1
                PT, PTt = ptb[pi_], TK(f"ptb{pi_}")
                OP("act", lambda e, pS=pS, PT=PT, h=h: e.activation(out=PT[:, 0:128], in_=pS[:, 0:128], func=AF.Exp, scale=0.125,
                                                                    bias=biasn[:, h:h + 1]), [pSt, TK("biasn")], [PTt])
                OP("pool", lambda e, PT=PT: e.tensor_tensor(out=PT[:, 0:128], in0=PT[:, 0:128], in1=maskb[:, :], op=ALU.mult),
                   [PTt, TK("maskb")], [PTt])
                MM(Nb[:, c * 128:(c + 1) * 128], [(Vn[:, 0, c, r0:r0 + 128], PT[:, 0:128])], [TK("Vn"), PTt], [Nt], start=(hh == 0 and c == 0), stop=False)
                MM(Db[:, c * 128:(c + 1) * 128], [(onespad[:, r0:r0 + 128], PT[:, 0:128])], [TK("onespad"), PTt], [Dt], start=(hh == 0 and c == 0), stop=False)
        ck_rows = cache_k.rearrange("l r d -> (l r) d")
        cv_rows = cache_v.rearrange("l r d -> (l r) d")
        cl_rows = cache_lf.rearrange("l r d -> (l r) d")
        eo_kv = li * NPG * 128 * 512
        eo_lf = li * NPG * 128 * 8
        for s in range(NS):
            for pg in range(NPAGES):
                col = s * NPAGES + pg
                S.dma("pool", lambda e, pg=pg, col=col: e.indirect_dma_start(
                    out=lfp[:, pg, :], out_offset=None, in_=cl_rows,
                    in_offset=bass.IndirectOffsetOnAxis(ap=idxall[:, col:col + 1], axis=0), element_offset=eo_lf), "lfp", [TK("idxall")], [TK("lfp")], nodeps=(pg > 0))
            pc, pct = ps_next("lo")
            lf2 = lfp[:, :, :].rearrange("p a b -> p (a b)")
            W = NPAGES * 8
            MM(pc[:, 0:W], [(utri[:, :], lf2)], [TK("lfp"), TK("utri")], [pct])
            MM(pc[:, 128:128 + W], [(onesf[:, :], lf2)], [TK("lfp"), TK("onesf")], [pct])
            OP("act", lambda e, pc=pc: e.activation(out=totA[:, :, :].rearrange("p a b -> p (a b)"), in_=pc[:, 128:128 + W], func=AF.Copy),
               [pct], [TK("totA")])
            for h in range(8):
                OP("dve", lambda e, h=h: e.tensor_tensor_scan(out=incl[:, :, h], data0=ones16[:, 0:NPAGES], data1=totA[:, :, h],
                                                              initial=0.0, op0=ALU.mult, op1=ALU.add),
                   [TK("totA"), TK("ones16")], [TK("incl")])
            OP("dve", lambda e, pc=pc: e.tensor_tensor(out=tb1[:, :, :].rearrange("p a b -> p (a b)"),
                                                       in0=totA[:, :, :].rearrange("p a b -> p (a b)"), in1=pc[:, 0:W], op=ALU.subtract),
               [TK("totA"), pct], [TK("tb1")])
            OP("dve", lambda e: e.tensor_tensor(out=tb1[:, :, :], in0=tb1[:, :, :], in1=incl[:, :, :], op=ALU.subtract),
               [TK("tb1"), TK("incl")], [TK("tb1")])
            OP("dve", lambda e: e.tensor_tensor(out=tb1[:, :, :], in0=tb1[:, :, :],
                                                in1=incl[:, NPAGES - 1:NPAGES, :].broadcast_to([128, NPAGES, 8]), op=ALU.add),
               [TK("tb1"), TK("incl")], [TK("tb1")])
            OP("dve", lambda e: e.tensor_scalar(out=biasp[:, :, :], in0=tb1[:, :, :], scalar1=negB[:, 0:1], scalar2=None, op0=ALU.add),
               [TK("tb1"), TK("negB")], [TK("biasp")])
            for pg in range(NPAGES):
                col = s * NPAGES + pg
                gi = grot3[0] % 2
                grot3[0] += 1
                kr, krt = kraw[gi], TK(f"kraw{gi}")
                vr_, vrt = vraw[gi], TK(f"vraw{gi}")
                S.dma("pool", lambda e, kr=kr, col=col: e.indirect_dma_start(
                    out=kr[:, :], out_offset=None, in_=ck_rows,
                    in_offset=bass.IndirectOffsetOnAxis(ap=idxall[:, col:col + 1], axis=0), element_offset=eo_kv), f"kraw{gi}", [TK("idxall")], [krt])
                S.dma("pool", lambda e, vr_=vr_, col=col: e.indirect_dma_start(
                    out=vr_[:, :], out_offset=None, in_=cv_rows,
                    in_offset=bass.IndirectOffsetOnAxis(ap=idxall[:, col:col + 1], axis=0), element_offset=eo_kv), f"vraw{gi}", [TK("idxall")], [vrt])
                bi = col % 2
                bk, bt = ps_next("lo")
                pb = bk[:].bitcast(BF16)

                def tr(e, pb=pb, kr=kr):
                    inst = None
                    for c in range(4):
                        inst = e.transpose(pb[:, c * 128:(c + 1) * 128], kr[:, c * 128:(c + 1) * 128], identb[:])
                    return inst
                OP("pe", tr, [krt, TK("identb")], [bt])
                OP("act", lambda e, pb=pb, bi=bi: e.activation(out=kTp[bi][:, :, :], in_=pb[:, 0:512].rearrange("p (c t) -> p c t", t=128),
                                                               func=AF.Copy), [bt], [TK(f"kTp{bi}")])
                v4 = vr_[:, :].rearrange("p (c two d) -> p c two d", two=2, d=64)
                OP("pool", lambda e, bi=bi, v4=v4: e.tensor_copy(out=vpad[bi][:, :, 0:64], in_=v4[:, :, 0, :]), [vrt], [TK(f"vpad{bi}")])
                OP("pool", lambda e, bi=bi, v4=v4: e.tensor_copy(out=vpad[bi][:, :, 128:192], in_=v4[:, :, 1, :]), [vrt], [TK(f"vpad{bi}")])
                pS, pSt = ps_next("lo")

                def qk(e, pS=pS, bi=bi, s=s):
                    inst = None
                    for h in range(8):
                        c, r0 = h // 2, (h % 2) * 64
                        inst = e.matmul(pS[:, h * 8:(h + 1) * 8], lhsT=kTp[bi][r0:r0 + 64, c, :], rhs=qT[r0:r0 + 64, c, s * 8:(s + 1) * 8],
                                        start=True, stop=True, skip_group_check=True)
                    return inst
                OP("pe", qk, [TK(f"kTp{bi}")] + qts, [pSt])
                OP("dve", lambda e, pS=pS, pg=pg: e.scalar_tensor_tensor(
                    out=sbs[:, :].rearrange("p (h q) -> p h q", q=8), in0=pS[:, 0:64].rearrange("p (h q) -> p h q", q=8), scalar=0.125,
                    in1=biasp[:, pg, :].unsqueeze(2).broadcast_to([128, 8, 8]), op0=ALU.mult, op1=ALU.add),
                   [pSt, TK("biasp")], [TK("sbs")])
                OP("act", lambda e, bi=bi: e.activation(out=pts[bi][:, :], in_=sbs[:, :], func=AF.Exp), [TK("sbs")], [TK(f"pts{bi}")])
                lastpg = (pg == NPAGES - 1)

                def pvm(e, bi=bi, s=s, lastpg=lastpg):
                    inst = None
                    for h in range(8):
                        c, r0 = h // 2, (h % 2) * 64
                        o = c * 128 + s * 8
                        st_ = lastpg and (h % 2 == 1)
                        e.matmul(Nb[:, o:o + 8], lhsT=vpad[bi][:, c, r0:r0 + 128], rhs=pts[bi][:, h * 8:(h + 1) * 8],
                                 start=False, stop=st_, skip_group_check=True)
                        inst = e.matmul(Db[:, o:o + 8], lhsT=onespad[:, r0:r0 + 128], rhs=pts[bi][:, h * 8:(h + 1) * 8],
                                        start=False, stop=st_, skip_group_check=True)
                    return inst
                OP("pe", pvm, [TK(f"vpad{bi}"), TK(f"pts{bi}"), TK("onespad")], [Nt, Dt])
        OP("dve", lambda e: e.reciprocal(out=rden[:, :], in_=Db[:, :]), [Dt], [TK("rden")])
        OP("dve", lambda e: e.tensor_tensor(out=mT[:, 0:4, 0:128], in0=Nb[:, :].rearrange("p (c t) -> p c t", t=128),
                                            in1=rden[:, :].rearrange("p (c t) -> p c t", t=128), op=ALU.mult),
           [Nt, TK("rden")], m_tiles(0, 128))

    def out_proj_add(ps_, Wd, nk):
        Wv = Wd.rearrange("(kc p) n -> p kc n", p=128)
        ws = []
        for k0 in range(0, nk, 4):
            ws.append(wload(Wv[:, k0:k0 + 4, :]))
        for tt in range(ps_.NT):
            cols = slice(tt * 128, (tt + 1) * 128)
            for half in range(2):
                bk, bt = ps_next()
                pairs = [(mT[:, kc, cols], ws[kc // 4][0][:, kc % 4, half * 512:(half + 1) * 512]) for kc in range(nk)]
                MM(bk[:, :], pairs, [TK(f"mT{tt}")] + [w[1] for w in ws], [bt])
                OP("dve", lambda e, bk=bk, tt=tt, half=half: e.tensor_tensor(out=x[:, tt, half * 512:(half + 1) * 512],
                                                                             in0=x[:, tt, half * 512:(half + 1) * 512], in1=bk[:, :], op=ALU.add),
                   [bt, TK(f"x{tt}")], [TK(f"x{tt}")])

    def ab_mixer(ps_, li):
        ab_setup(li)
        if ps_.sample:
            load_rows_T(rgtail_s, TK("rgtail_s"), st_rg_conv[li], NS * 12)
            load_rows_T(rghst_s, TK("rghst_s"), st_rg_h[li], NS * 4)
        ab_tokmajor(ps_, li)
        ab_rg(ps_, li)
        if ps_.sample:
            attn_sample(ps_, li)
            store_rows_T(rgtail_so, TK("rgtail_s"), rg_conv_s[li], NS * 12)
            store_rows_T(rghst_so, TK("rghst_s"), rg_h_s[li], NS * 4)
        else:
            attn_prompt(ps_, li)
            if ps_.chunk == NPASS - 1:
                store_rows_T(rgtail[li], TK(f"rgtail{li}"), rg_conv_p[li], 12)
                store_rows_T(rghst[li], TK(f"rghst{li}"), rg_h_p[li], 4)
        out_proj_add(ps_, ab_w_out[li], 8)

    def pool_mixer(ps_, li):
        smp = ps_.sample
        if smp:
            load_rows_T(ptail_s, TK("ptail_s"), st_pool[li], NS * 120)
            OUT(pool_s[li][:, 0:7, :], st_pool[li].rearrange("(s j kc) p -> s j (kc p)", j=15, kc=8)[:, 8:15, :])
            for s_ in range(NS):
                OUT(pool_s[li][s_, 7:15, :], xnf[s_ * 8:(s_ + 1) * 8, :], "xnf")
        elif ps_.chunk == NPASS - 1:
            OUT(pool_p[li][:, :], xnf[113:128, :], "xnf")
        pw, pwt = wload(pool_w[li].rearrange("g (kk p) n -> p (g kk) n", p=128))
        psc, psct = gload(pool_scale[li:li + 1, :])

        def ev(buf, lo, hi):
            if smp:
                return buf[:, 0:NS * 23].rearrange("p (s t) -> p s t", t=23)[:, :, lo:hi]
            return buf[:, lo:hi]
        for g, w in enumerate(POOL_WINDOWS):
            for kk in range(2):
                kc = 2 * g + kk
                for (col0, n) in ps_.blocks:
                    L = DL if smp else n
                    E = 15 + L
                    if smp:
                        hv = ptail_s[:, :].rearrange("p (s j k) -> p s j k", j=15, k=8)[:, :, :, kc]
                        OP("pool", lambda e, hv=hv: e.tensor_copy(out=ev(pe0, 0, 15), in_=hv), [TK("ptail_s")], [TK("pe0")])
                        dsrc = xnT[:, kc, 0:n].rearrange("p (s t) -> p s t", t=DL)
                    else:
                        hv = ptail[li][:, :].rearrange("p (j k) -> p j k", k=8)[:, :, kc]
                        OP("pool", lambda e, hv=hv: e.tensor_copy(out=pe0[:, 0:15], in_=hv), [TK(f"ptail{li}")], [TK("pe0")])
                        dsrc = xnT[:, kc, col0:col0 + n]
                    OP("pool", lambda e, dsrc=dsrc, E=E: e.tensor_copy(out=ev(pe0, 15, E), in_=dsrc), xn_tiles(ps_, col0, n), [TK("pe0")])
                    src, stok = pe0, TK("pe0")
                    step = 1
                    k = 0
                    while step < w:
                        dst, dtok = (peA, TK("peA")) if k % 2 == 0 else (peB, TK("peB"))
                        lo = 2 * step - 1
                        OP("dve", lambda e, src=src, dst=dst, lo=lo, step=step, E=E: e.tensor_tensor(
                            out=ev(dst, lo, E), in0=ev(src, lo, E), in1=ev(src, lo - step, E - step), op=ALU.add), [stok], [dtok])
                        src, stok = dst, dtok
                        step *= 2
                        k += 1
                    dcols = mT[:, kc, 0:n].rearrange("p (s t) -> p s t", t=DL) if smp else mT[:, kc, col0:col0 + n]
                    OP("dve", lambda e, src=src, dcols=dcols, w=w, E=E: e.scalar_tensor_tensor(
                        out=dcols, in0=ev(src, 15, E), scalar=1.0 / w, in1=ev(pe0, 15, E), op0=ALU.mult, op1=ALU.subtract),
                       [stok, TK("pe0")], m_tiles(col0, n))
                    if (not smp) and ps_.chunk == 0 and col0 == 0:
                        OP("dve", lambda e, src=src, w=w: e.tensor_tensor(out=ptmp[:, 0:w - 1], in0=src[:, 15:15 + w - 1],
                                                                          in1=invcnt[:, 0:w - 1], op=ALU.mult), [stok, TK("invcnt")], [TK("ptmp")])
                        OP("dve", lambda e, kc=kc, w=w: e.tensor_tensor(out=mT[:, kc, 0:w - 1], in0=ptmp[:, 0:w - 1], in1=pe0[:, 15:15 + w - 1],
                                                                        op=ALU.subtract), [TK("ptmp"), TK("pe0")], m_tiles(0, 128))
                    if not smp:
                        OP("pool", lambda e, hv=hv, n=n: e.tensor_copy(out=hv, in_=pe0[:, n:n + 15]), [TK("pe0")], [TK(f"ptail{li}")])
        for tt in range(ps_.NT):
            cols = slice(tt * 128, (tt + 1) * 128)
            for half in range(2):
                bk, bt = ps_next()
                for gg in range(2):
                    g = half * 2 + gg
                    MM(bk[:, gg * 256:(gg + 1) * 256], [(mT[:, 2 * g + kk, cols], pw[:, 2 * g + kk, :]) for kk in range(2)],
                       [TK(f"mT{tt}"), pwt], [bt])
                OP("dve", lambda e, bk=bk, half=half: e.tensor_tensor(out=ptmp[:, :], in0=bk[:, :], in1=psc[:, half * 512:(half + 1) * 512],
                                                                      op=ALU.mult), [bt, psct], [TK("ptmp")])
                OP("pool", lambda e, tt=tt, half=half: e.tensor_tensor(out=x[:, tt, half * 512:(half + 1) * 512],
                                                                       in0=x[:, tt, half * 512:(half + 1) * 512], in1=ptmp[:, :], op=ALU.add),
                   [TK("ptmp"), TK(f"x{tt}")], [TK(f"x{tt}")])

    frot = [0]

    def ffn(ps_, l):
        smp = ps_.sample
        fp_ = fpar[l]
        fpt = TK(f"fpar{l}")
        Wu = ffn_w_up[l].rearrange("(kc p) (two f) -> p kc two f", p=128, two=2)
        if smp:
            load_rows_T(ftail_s, TK("ftail_s"), st_ffn[l], NS * 96)
        last = (not smp) and ps_.chunk == NPASS - 1
        wcur = None
        for fc in range(24):
            if fc % 2 == 0:
                wcur = wload(Wu[:, :, :, fc * 128:(fc + 2) * 128])
            wu, wut = wcur
            fo = (fc % 2) * 128
            for (col0, n) in ps_.blocks:
                xt = xn_tiles(ps_, col0, n)
                fs = frot[0] % NFS
                frot[0] += 1
                for br in range(2):
                    ch = br * 24 + fc
                    ue = fue[fs][br]
                    uet = TK(f"fue{fs}_{br}")
                    ac = facc[fs][br]
                    act_ = TK(f"facc{fs}_{br}")
                    if smp:
                        hv = ftail_s[:, :].rearrange("p (s j f) -> p s j f", j=2, f=48)[:, :, :, ch]
                        e3 = ue[:, 0:NS * 10].rearrange("p (s t) -> p s t", t=10)
                        OP("pool", lambda e, hv=hv, e3=e3: e.tensor_copy(out=e3[:, :, 0:2], in_=hv), [TK("ftail_s")], [uet])
                    else:
                        hv = ftail[l][:, :].rearrange("p (j f) -> p j f", f=48)[:, :, ch]
                        OP("pool", lambda e, hv=hv, ue=ue: e.tensor_copy(out=ue[:, 0:2], in_=hv), [TK(f"ftail{l}_{ch}")], [uet])
                    bk, bt = ps_next()
                    MM(bk[:, 0:n], [(wu[:, kc, br, fo:fo + 128], xnT[:, kc, col0:col0 + n]) for kc in range(8)], xt + [wut], [bt])
                    bv = bk[:, 0:n].rearrange("p (s t) -> p s t", t=DL) if smp else bk[:, 0:n]
                    OP("act", lambda e, bv=bv, ue=ue, n=n: e.activation(out=eview(ue, ps_, 2, 0, n, 0), in_=bv, func=AF.Copy), [bt], [uet])
                    OP("act", lambda e, ue=ue, ac=ac, n=n, ch=ch: e.activation(
                        out=cview(ac, ps_, 0, n), in_=eview(ue, ps_, 2, 0, n, 0), func=AF.Identity,
                        scale=fp_[:, 96 + ch:97 + ch], bias=fp_[:, 144 + ch:145 + ch]), [uet, fpt], [act_])
                    for j in (1, 0):
                        OP("dve", lambda e, ue=ue, ac=ac, n=n, ch=ch, j=j: e.scalar_tensor_tensor(
                            out=cview(ac, ps_, 0, n), in0=eview(ue, ps_, 2, 0, n, j - 2), scalar=fp_[:, j * 48 + ch:j * 48 + ch + 1],
                            in1=cview(ac, ps_, 0, n), op0=ALU.mult, op1=ALU.add), [uet, act_, fpt], [act_])
                    if smp:
                        OP("pool", lambda e, e3=e3, hv=hv: e.tensor_copy(out=hv, in_=e3[:, :, 8:10]), [uet], [TK("ftail_s")])
                    else:
                        OP("pool", lambda e, hv=hv, ue=ue, n=n: e.tensor_copy(out=hv, in_=ue[:, n:n + 2]), [uet], [TK(f"ftail{l}_{ch}")])
                OP("act", lambda e, n=n, fs=fs: e.activation(out=fgl[fs][:, 0:n], in_=facc[fs][0][:, 0:n], func=AF.Gelu_apprx_tanh),
                   [TK(f"facc{fs}_0")], [TK(f"fgl{fs}")])
                OP("pool", lambda e, n=n, fc=fc, col0=col0, fs=fs: e.tensor_tensor(out=hT[:, fc, col0:col0 + n], in0=fgl[fs][:, 0:n],
                                                                                   in1=facc[fs][1][:, 0:n], op=ALU.mult),
                   [TK(f"fgl{fs}"), TK(f"facc{fs}_1")], [TK(f"hT{fc}_{col0}")])
        if smp:
            store_rows_T(ftail_s, TK("ftail_s"), ffn_conv_s[l], NS * 96)
        elif last:
            OP("pool", lambda e, l=l: e.tensor_copy(out=ftail[l][:, 0:1], in_=ftail[l][:, 0:1]),
               [TK(f"ftail{l}_{ch}") for ch in range(48)], [TK(f"ftail{l}")])
            store_rows_T(ftail[l], TK(f"ftail{l}"), ffn_conv_p[l], 96)
        Wd = ffn_w_down[l].rearrange("(fc p) n -> p fc n", p=128)
        for sg in range(6):
            wd, wdt = wload(Wd[:, sg * 4:(sg + 1) * 4, :])
            for tt in range(ps_.NT):
                cols = slice(tt * 128, (tt + 1) * 128)
                c0 = (tt * 128) // 512 * 512 if not smp else 0
                for half in range(2):
                    bk, bt = ps_next()
                    MM(bk[:, :], [(hT[:, sg * 4 + j, cols], wd[:, j, half * 512:(half + 1) * 512]) for j in range(4)],
                       [TK(f"hT{sg * 4 + j}_{c0}") for j in range(4)] + [wdt], [bt])
                    OP("dve", lambda e, bk=bk, tt=tt, half=half: e.tensor_tensor(out=x[:, tt, half * 512:(half + 1) * 512],
                                                                                 in0=x[:, tt, half * 512:(half + 1) * 512], in1=bk[:, :], op=ALU.add),
                       [bt, TK(f"x{tt}")], [TK(f"x{tt}")])

    def run_pass(ps_):
        for tt in range(ps_.NT):
            DMA("sp", x[:, tt, :], ps_.x_src[tt * 128:(tt + 1) * 128, :], f"xin{tt}", writes=[TK(f"x{tt}")])
        for l in range(DEPTH):
            li = l // 2
            S.fence()
            if l % 2 == 0:
                norm(ps_, norm_mix[l:l + 1, :])
                ab_mixer(ps_, li)
            else:
                norm(ps_, norm_mix[l:l + 1, :], want_f32_last=True)
                pool_mixer(ps_, li)
            S.fence()
            norm(ps_, norm_ffn[l:l + 1, :])
            ffn(ps_, l)
        for tt in range(ps_.NT):
            OUT(ps_.y_dst[tt * 128:(tt + 1) * 128, :], x[:, tt, :], f"x{tt}")

    consts()
    for li in range(NAB):
        OP("pool", lambda e, li=li: e.memset(rgtail[li][:], 0.0), [], [TK(f"rgtail{li}")])
        OP("pool", lambda e, li=li: e.memset(rghst[li][:], 0.0), [], [TK(f"rghst{li}")])
    for li in range(NPL):
        OP("pool", lambda e, li=li: e.memset(ptail[li][:], 0.0), [], [TK(f"ptail{li}")])
    for l in range(DEPTH):
        OP("pool", lambda e, l=l: e.memset(ftail[l][:], 0.0), [], [TK(f"ftail{l}_{ch}") for ch in range(48)])
    passes = []
    for i in range(NPASS):
        p_ = Pass()
        p_.sample = False
        p_.chunk = i
        p_.T = CH
        p_.L = CH
        p_.NT = CH // 128
        p_.gt0 = i * (CH // 128)
        p_.blocks = [(0, 512), (512, 512)]
        p_.x_src = xp[i * CH:(i + 1) * CH, :]
        p_.y_dst = y_p[i * CH:(i + 1) * CH, :]
        p_.k_out = lambda li, i=i: k_p[li][i * CH:(i + 1) * CH, :]
        p_.v_out = lambda li, i=i: v_p[li][i * CH:(i + 1) * CH, :]
        p_.lf_out = lambda li, i=i: logf_p[li][i * CH:(i + 1) * CH, :]
        passes.append(p_)
    sp_ = Pass()
    sp_.sample = True
    sp_.chunk = 0
    sp_.T = 128
    sp_.L = DL
    sp_.NT = 1
    sp_.gt0 = 0
    sp_.blocks = [(0, 128)]
    sp_.x_src = xs
    sp_.y_dst = y_s
    sp_.k_out = lambda li: k_s[li]
    sp_.v_out = lambda li: v_s[li]
    sp_.lf_out = lambda li: logf_s[li]
    for pi_, p_ in enumerate(passes):
        wpass[0] = pi_
        wseq[0] = 0
        run_pass(p_)
    S.fence()
    wpass[0] = len(passes)
    wseq[0] = 0
    run_pass(sp_)
    print("total recorded", S.count)
    print("SBUF arenas: A", arA.hi, "B", arB.hi, "S", arS.hi, "ops", {e: len(S.ops[e]) for e in ENGS})
    S.emit(nc, st, final_keys=sorted(out_keys))
    st.close()
    return nc


_NC_CACHE = {}


def _get_nc(cfg_key):
    if cfg_key not in _NC_CACHE:
        _NC_CACHE[cfg_key] = build(Cfg(*cfg_key))
    return _NC_CACHE[cfg_key]


def kernel(x_prompt, x_sample, cache_k, cache_v, cache_logf, state_rg_h, state_rg_conv, state_pool,
           state_ffn_conv, page_table, norm_mix, norm_ffn, ab_w_in, ab_b_f, ab_q_gain, ab_k_gain,
           ab_conv_w, ab_conv_b, ab_w_a, ab_b_a, ab_w_x, ab_b_x, ab_lambda, ab_w_out, pool_w, pool_scale,
           ffn_w_up, ffn_conv_w, ffn_conv_b, ffn_w_down):
    f = lambda a: np.ascontiguousarray(np.asarray(a))
    x_prompt, x_sample = f(x_prompt), f(x_sample)
    B, SEQ, D = x_prompt.shape
    DB, DL, _ = x_sample.shape
    DEPTH = norm_mix.shape[0]
    NAB = (DEPTH + 1) // 2
    NPL = DEPTH // 2
    NPG = cache_k.shape[1]
    NPAGES = page_table.shape[1]
    n = 8
    NS = DB // n
    cfg_key = (SEQ, DEPTH, NPG, NPAGES)
    nc = _get_nc(cfg_key)
    ck = f(cache_k).reshape(NAB, NPG * 128, 512)
    cv = f(cache_v).reshape(NAB, NPG * 128, 512)
    cl = f(cache_logf).reshape(NAB, NPG * 128, 8)
    ab_par = np.concatenate([f(ab_conv_w).reshape(NAB, 16, 128), f(ab_conv_b).reshape(NAB, 4, 128),
                             f(ab_b_a).reshape(NAB, 4, 128), f(ab_b_x).reshape(NAB, 4, 128),
                             f(ab_lambda).reshape(NAB, 4, 128)], axis=1)
    ffn_par = np.concatenate([f(ffn_conv_w).reshape(DEPTH, 144, 128), f(ffn_conv_b).reshape(DEPTH, 48, 128)], axis=1)
    pw = f(pool_w) if NPL else np.zeros((1, 4, 256, 256), np.float32)
    psc = f(pool_scale) if NPL else np.zeros((1, 1024), np.float32)
    shared = {
        "cache_k": ck, "cache_v": cv, "cache_lf": cl,
        "norm_mix": f(norm_mix), "norm_ffn": f(norm_ffn), "ab_w_in": f(ab_w_in), "ab_b_f": f(ab_b_f),
        "ab_q_gain": f(ab_q_gain), "ab_k_gain": f(ab_k_gain), "ab_par": ab_par, "ab_w_a": f(ab_w_a), "ab_w_x": f(ab_w_x),
        "ab_w_out": f(ab_w_out), "pool_w": pw, "pool_scale": psc, "ffn_w_up": f(ffn_w_up), "ffn_par": ffn_par,
        "ffn_w_down": f(ffn_w_down),
    }
    srh, src_, spl, sff, ptb_ = f(state_rg_h), f(state_rg_conv), f(state_pool), f(state_ffn_conv), f(page_table)
    in_maps = []
    for c in range(n):
        sl = slice(c * NS, (c + 1) * NS)
        m = dict(shared)
        m["xp"] = x_prompt[c % B]
        m["xs"] = x_sample[sl].reshape(NS * DL, D)
        m["st_rg_h"] = srh[:, sl].reshape(NAB, NS * 4, 128)
        m["st_rg_conv"] = src_[:, sl].reshape(NAB, NS * 12, 128)
        m["st_pool"] = spl[:, sl].reshape(NPL, NS * 120, 128) if NPL else np.zeros((1, NS * 120, 128), np.float32)
        m["st_ffn"] = sff[:, sl].reshape(DEPTH, NS * 96, 128)
        m["page_table"] = ptb_[sl].reshape(1, NS * NPAGES).astype(np.int32)
        in_maps.append(m)
    res = run_bass_kernel_spmd(nc, in_maps, core_ids=list(range(n)))
    R = res.results
    cat = lambda name, ax: np.concatenate([R[c][name] for c in range(n)], axis=ax)
    y_prompt = np.stack([R[b]["y_p"] for b in range(B)])
    y_sample = cat("y_s", 0).reshape(DB, DL, D)
    k_p = np.stack([R[b]["k_p"] for b in range(B)], axis=1).reshape(NAB, B, SEQ, 8, 64)
    v_p = np.stack([R[b]["v_p"] for b in range(B)], axis=1).reshape(NAB, B, SEQ, 8, 64)
    logf_p = np.stack([R[b]["logf_p"] for b in range(B)], axis=1)
    rg_h_p = np.stack([R[b]["rg_h_p"] for b in range(B)], axis=1).reshape(NAB, B, 512)
    rg_conv_p = np.stack([R[b]["rg_conv_p"] for b in range(B)], axis=1).reshape(NAB, B, 3, 512)
    pool_p = np.stack([R[b]["pool_p"] for b in range(B)], axis=1)[:NPL]
    ffn_conv_p = np.stack([R[b]["ffn_conv_p"] for b in range(B)], axis=1).reshape(DEPTH, B, 2, 6144)
    k_s = cat("k_s", 1).reshape(NAB, DB, DL, 8, 64)
    v_s = cat("v_s", 1).reshape(NAB, DB, DL, 8, 64)
    logf_s = cat("logf_s", 1).reshape(NAB, DB, DL, 8)
    rg_h_s = cat("rg_h_s", 1).reshape(NAB, DB, 512)
    rg_conv_s = cat("rg_conv_s", 1).reshape(NAB, DB, 3, 512)
    pool_s = cat("pool_s", 1)[:NPL]
    ffn_conv_s = cat("ffn_conv_s", 1).reshape(DEPTH, DB, 2, 6144)
    return (y_prompt, y_sample, k_p, v_p, logf_p, rg_h_p, rg_conv_p, pool_p, ffn_conv_p,
            k_s, v_s, logf_s, rg_h_s, rg_conv_s, pool_s, ffn_conv_s)
```

```python
import numpy as np
import concourse.bass as bass
import concourse.mybir as mybir
from concourse.bass_utils import run_bass_kernel_spmd
from contextlib import ExitStack

F32 = mybir.dt.float32
BF16 = mybir.dt.bfloat16
I32 = mybir.dt.int32
AF = mybir.ActivationFunctionType
ALU = mybir.AluOpType
AX = mybir.AxisListType

EPS = 1e-6
POOL_WINDOWS = (2, 4, 8, 16)


class Tok:
    __slots__ = ("w", "rs", "name")

    def __init__(self, name=""):
        self.w = None
        self.rs = []
        self.name = name


class Node:
    __slots__ = ("eng", "fn", "waits", "needed", "sigval", "idx", "dkey", "dval")

    def __init__(self, eng, fn):
        self.eng = eng
        self.fn = fn
        self.waits = []
        self.needed = False
        self.sigval = None
        self.idx = None
        self.dkey = None
        self.dval = None


ENGS = ("pe", "act", "dve", "pool", "sp")


class Sched:
    def __init__(self):
        self.ops = {e: [] for e in ENGS}
        self.seen = {e: {} for e in ENGS}
        self.dcnt = {}

    def _deps(self, eng, reads, writes):
        deps = []
        for t in reads:
            if t.w is not None:
                deps.append(t.w)
        for t in writes:
            if t.w is not None:
                deps.append(t.w)
            deps.extend(t.rs)
        best = {}
        for d in deps:
            if d.dkey is not None:
                k = ("d", d.dkey)
                v = d.dval
            else:
                if d.eng == "pe" and eng == "pe":
                    continue
                k = ("e", d.eng)
                v = d.idx
            if k not in best or best[k][0] < v:
                best[k] = (v, d)
        out = []
        seen = self.seen[eng]
        for k, (v, d) in best.items():
            if seen.get(k, -1) >= v:
                continue
            seen[k] = v
            if d.dkey is None:
                d.needed = True
            out.append(d)
        return out

    def _commit(self, node, reads, writes):
        for t in reads:
            if len(t.rs) > 24:
                last = {}
                for r in t.rs:
                    last[(r.eng, r.dkey)] = r
                t.rs = list(last.values())
            t.rs.append(node)
        for t in writes:
            t.w = node
            t.rs = []

    limit = None
    count = 0

    def op(self, eng, fn, reads=(), writes=()):
        self.count += 1
        if self.limit is not None and self.count > self.limit:
            return Node(eng, fn)
        node = Node(eng, fn)
        node.waits = self._deps(eng, reads, writes)
        node.idx = len(self.ops[eng])
        self.ops[eng].append(node)
        self._commit(node, reads, writes)
        return node

    def _deps_noprune(self, eng, reads, writes):
        deps = []
        for t in reads:
            if t.w is not None:
                deps.append(t.w)
        for t in writes:
            if t.w is not None:
                deps.append(t.w)
            deps.extend(t.rs)
        best = {}
        for d in deps:
            if d.dkey is not None:
                k, v = ("d", d.dkey), d.dval
            else:
                k, v = ("e", d.eng), d.idx
            if k not in best or best[k][0] < v:
                best[k] = (v, d)
        out = []
        for k, (v, d) in best.items():
            if d.dkey is None:
                d.needed = True
            out.append(d)
        return out

    def dma(self, q, fn, key, reads=(), writes=(), nodeps=False, before=None):
        self.count += 1
        if self.limit is not None and self.count > self.limit:
            return Node(q, fn)
        node = Node(q, fn)
        if nodeps:
            node.waits = []
        elif before is not None:
            node.waits = self._deps_noprune(q, reads, writes)
        else:
            node.waits = self._deps(q, reads, writes)
        node.idx = len(self.ops[q])
        node.dkey = key
        self.dcnt[key] = self.dcnt.get(key, 0) + 16
        node.dval = self.dcnt[key]
        if before is not None:
            pos = len(self.ops[q]) - 1
            while pos >= 0 and self.ops[q][pos] is not before:
                pos -= 1
            assert pos >= 0
            self.ops[q].insert(pos, node)
        else:
            self.ops[q].append(node)
        self._commit(node, reads, writes)
        return node

    def fence(self):
        lasts = {e: (self.ops[e][-1] if self.ops[e] else None) for e in ENGS}
        for e in ENGS:
            for n in reversed(self.ops[e]):
                if n.fn is not None and n.dkey is None:
                    lasts[e] = n
                    break
            else:
                lasts[e] = None
        for e in ENGS:
            node = Node(e, None)
            seen = self.seen[e]
            for x in ENGS:
                d = lasts[x]
                if x == e or d is None:
                    continue
                k = ("e", x)
                if seen.get(k, -1) >= d.idx:
                    continue
                seen[k] = d.idx
                d.needed = True
                node.waits.append(d)
            for key, cnt in self.dcnt.items():
                k = ("d", key)
                if seen.get(k, -1) >= cnt:
                    continue
                seen[k] = cnt
                pd = Node("sp", None)
                pd.dkey = key
                pd.dval = cnt
                node.waits.append(pd)
            node.idx = len(self.ops[e])
            self.ops[e].append(node)

    def emit(self, nc, stack, final_keys=()):
        esem = {e: stack.enter_context(nc.semaphore("s_" + e)) for e in ENGS}
        dsem = {k: stack.enter_context(nc.semaphore("d_" + str(k))) for k in self.dcnt}
        for e in ENGS:
            c = 0
            for n in self.ops[e]:
                if n.needed and n.dkey is None:
                    c += 1
                    n.sigval = c
        block = stack.enter_context(nc.Block())

        def run(eng_name):
            def body(eng):
                for n in self.ops[eng_name]:
                    for d in n.waits:
                        if d.dkey is not None:
                            eng.wait_ge(dsem[d.dkey], d.dval)
                        else:
                            eng.wait_ge(esem[d.eng], d.sigval)
                    if n.fn is None:
                        continue
                    inst = n.fn(eng)
                    if n.dkey is not None:
                        inst.then_inc(dsem[n.dkey], 16)
                    elif n.needed:
                        inst.then_inc(esem[eng_name], 1)
                if eng_name == "sp":
                    for k in final_keys:
                        if k in self.dcnt:
                            eng.wait_ge(dsem[k], self.dcnt[k])
            return body

        block.tensor(run("pe"))
        block.scalar(run("act"))
        block.vector(run("dve"))
        block.gpsimd(run("pool"))
        block.sync(run("sp"))


class Cfg:
    def __init__(self, SEQ=4096, DEPTH=4, NPG=2560, NPAGES=16):
        self.SEQ = SEQ
        self.DEPTH = DEPTH
        self.NPG = NPG
        self.NPAGES = NPAGES
        self.NAB = (DEPTH + 1) // 2
        self.NPL = DEPTH // 2
        self.CH = 1024
        self.NPASS = SEQ // self.CH
        self.NS = 16
        self.DL = 8


class Pass:
    pass


def build(cfg):
    SEQ, DEPTH, NAB, NPL, CH, NPASS = cfg.SEQ, cfg.DEPTH, cfg.NAB, cfg.NPL, cfg.CH, cfg.NPASS
    NS, DL, NPAGES, NPG = cfg.NS, cfg.DL, cfg.NPAGES, cfg.NPG
    NTG = SEQ // 128
    nc = bass.Bass("TRN2", target_bir_lowering=False)
    S = Sched()
    import os as _os
    if _os.environ.get("DBG_LIMIT"):
        S.limit = int(_os.environ["DBG_LIMIT"])
    st = ExitStack()

    def din(name, shape, dt=F32):
        return nc.dram_tensor(name, list(shape), dt, kind="ExternalInput").ap()

    def dout(name, shape, dt=F32):
        return nc.dram_tensor(name, list(shape), dt, kind="ExternalOutput").ap()

    def dscr(name, shape, dt=BF16):
        return nc.dram_tensor(name, list(shape), dt, kind="Internal").ap()

    xp = din("xp", [SEQ, 1024])
    xs = din("xs", [128, 1024])
    cache_k = din("cache_k", [NAB, NPG * 128, 512])
    cache_v = din("cache_v", [NAB, NPG * 128, 512])
    cache_lf = din("cache_lf", [NAB, NPG * 128, 8])
    st_rg_h = din("st_rg_h", [NAB, NS * 4, 128])
    st_rg_conv = din("st_rg_conv", [NAB, NS * 12, 128])
    st_pool = din("st_pool", [max(NPL, 1), NS * 15 * 8, 128])
    st_ffn = din("st_ffn", [DEPTH, NS * 96, 128])
    page_table = din("page_table", [1, NS * NPAGES], I32)
    norm_mix = din("norm_mix", [DEPTH, 1024])
    norm_ffn = din("norm_ffn", [DEPTH, 1024])
    ab_w_in = din("ab_w_in", [NAB, 1024, 2568])
    ab_b_f = din("ab_b_f", [NAB, 8])
    ab_q_gain = din("ab_q_gain", [NAB, 64])
    ab_k_gain = din("ab_k_gain", [NAB, 64])
    ab_par = din("ab_par", [NAB, 32, 128])
    ab_w_a = din("ab_w_a", [NAB, 8, 64, 64])
    ab_w_x = din("ab_w_x", [NAB, 8, 64, 64])
    ab_w_out = din("ab_w_out", [NAB, 1024, 1024])
    pool_w = din("pool_w", [max(NPL, 1), 4, 256, 256])
    pool_scale = din("pool_scale", [max(NPL, 1), 1024])
    ffn_w_up = din("ffn_w_up", [DEPTH, 1024, 6144])
    ffn_par = din("ffn_par", [DEPTH, 192, 128])
    ffn_w_down = din("ffn_w_down", [DEPTH, 3072, 1024])

    y_p = dout("y_p", [SEQ, 1024])
    y_s = dout("y_s", [128, 1024])
    k_p = dout("k_p", [NAB, SEQ, 512])
    v_p = dout("v_p", [NAB, SEQ, 512])
    logf_p = dout("logf_p", [NAB, SEQ, 8])
    rg_h_p = dout("rg_h_p", [NAB, 4, 128])
    rg_conv_p = dout("rg_conv_p", [NAB, 12, 128])
    pool_p = dout("pool_p", [max(NPL, 1), 15, 1024])
    ffn_conv_p = dout("ffn_conv_p", [DEPTH, 96, 128])
    k_s = dout("k_s", [NAB, 128, 512])
    v_s = dout("v_s", [NAB, 128, 512])
    logf_s = dout("logf_s", [NAB, 128, 8])
    rg_h_s = dout("rg_h_s", [NAB, NS * 4, 128])
    rg_conv_s = dout("rg_conv_s", [NAB, NS * 12, 128])
    pool_s = dout("pool_s", [max(NPL, 1), NS, 15, 1024])
    ffn_conv_s = dout("ffn_conv_s", [DEPTH, NS * 96, 128])

    kT_scr = dscr("kT_scr", [NAB, 4, 128, SEQ])
    V_scr = dscr("V_scr", [NAB, 128, NTG, 4, 192])

    def sb(name, shape, dt=F32):
        return st.enter_context(nc.sbuf_tensor(name, list(shape), dt))

    toks = {}

    def TK(name):
        if name not in toks:
            toks[name] = Tok(name)
        return toks[name]

    class Arena:
        def __init__(self, base2d, nbytes):
            self.base = base2d
            self.n = nbytes
            self.off = 0
            self.hi = 0

        def reset(self):
            self.off = 0

        def get(self, shape, dt=F32):
            esz = 4 if dt in (F32, I32) else 2
            cnt = int(np.prod(shape[1:]))
            nb = (cnt * esz + 3) // 4 * 4
            assert self.off + nb <= self.n, ("arena overflow", self.off, nb, self.n, shape)
            v = self.base[:, self.off // 4:(self.off + nb) // 4]
            self.off += nb
            self.hi = max(self.hi, self.off)
            if dt == BF16:
                v = v.bitcast(BF16)[:, 0:cnt]
            elif dt == I32:
                v = v.bitcast(I32)
            if len(shape) == 3:
                v = v.rearrange("p (a b) -> p a b", b=shape[2])
            elif len(shape) == 4:
                v = v.rearrange("p (a b c) -> p a b c", b=shape[2], c=shape[3])
            return v

    x = sb("x", [128, 8, 1024])
    xnT = sb("xnT", [128, 8, 1024], BF16)
    mT = sb("mT", [128, 8, 1024], BF16)
    arA_t = sb("arA", [128, 49152 // 4])
    arB_t = sb("arB", [128, 38400 // 4])
    arA = Arena(arA_t[:, :], 49152)
    arB = Arena(arB_t[:, :], 38400)
    arS = Arena(x[:, 1:8, :].rearrange("p a b -> p (a b)"), 7 * 4096)
    wsl = [sb(f"wsl{i}", [128, 4096], BF16) for i in range(3)]
    wfb = sb("wfb", [128, 8, 8], BF16)
    gsl = [sb("gsl0", [128, 1024])]
    xnb = [sb(f"xnb{i}", [128, 1024], BF16) for i in range(2)]
    ss = sb("ss", [128, 8])
    rs = sb("rs", [128, 8])
    identb = sb("identb", [128, 128], BF16)
    identf = sb("identf", [128, 128])
    utri = sb("utri", [128, 128])
    onesf = sb("onesf", [128, 128])
    buf_ = sb("buf_", [128, 128])
    maskc = sb("maskc", [128, 128], BF16)
    maskb = sb("maskb", [128, 128], BF16)
    onespad = sb("onespad", [128, 192], BF16)
    ones16 = sb("ones16", [128, 16])
    invcnt = sb("invcnt", [128, 16])
    lfall = [sb(f"lfall{i}", [128, NTG, 8]) for i in range(NAB)]
    ckall = [sb(f"ckall{i}", [128, NTG, 8]) for i in range(NAB)]
    cendall = [sb(f"cendall{i}", [128, NTG + 1, 8]) for i in range(NAB)]
    biasall = sb("biasall", [128, 2, NTG, 8])
    crefB = sb("crefB", [128, 2, 8])
    gq = sb("gq", [128, 64])
    gk = sb("gk", [128, 64])
    bfb = sb("bfb", [128, 8])
    gmx = sb("gmx", [128, 2])
    negB = sb("negB", [128, 1])
    apar = [sb(f"apar{i}", [128, 32]) for i in range(NAB)]
    spn = sb("spn", [128, 8])
    wabd = sb("wabd", [128, 4, 128], BF16)
    wxbd = sb("wxbd", [128, 4, 128], BF16)
    rgtail = [sb(f"rgtail{i}", [128, 12]) for i in range(NAB)]
    rghst = [sb(f"rghst{i}", [128, 4]) for i in range(NAB)]
    ptail = [sb(f"ptail{i}", [128, 120]) for i in range(max(NPL, 1))]
    fpar = [sb(f"fpar{i}", [128, 192]) for i in range(DEPTH)]
    ftail = [sb(f"ftail{i}", [128, 96]) for i in range(DEPTH)]
    ptab = sb("ptab", [128, NS * NPAGES], I32)
    idxall = ptab
    rowbuf = sb("rowbuf", [128, 4, 128])
    rowo = [sb(f"rowo{i}", [128, 128]) for i in range(2)]
    hT = arA.get([128, 24, 1024], BF16)
    arA.reset()
    qT = arA.get([128, 4, 1024], BF16)
    kTn = arA.get([128, 4, 1024], BF16)
    Vn = arA.get([128, 8, 4, 192], BF16)
    kcache = [arA.get([128, max(SEQ - CH, 128)], BF16)] * 2
    vcache = [arA.get([128, max(NTG - 8, 1), 192], BF16)] * 2
    NPT = 6
    ptb = [arB.get([128, 512], BF16) for i in range(NPT)]
    sq = arB.get([128, 512])
    tmpf = arB.get([128, 512])
    ssq = arB.get([128, 16])
    rq = arB.get([128, 16])
    qb = arB.get([128, 512], BF16)
    kb = arB.get([128, 512], BF16)
    kf = [arB.get([128, 512]) for i in range(2)]
    vf = [arB.get([128, 512]) for i in range(2)]
    t8 = arB.get([128, 8])
    e8 = arB.get([128, 8])
    rden = arB.get([128, 512])
    rge = arB.get([128, 516])
    rgacc = arB.get([128, 512])
    rgxb = arB.get([128, 512], BF16)
    rga = arB.get([128, 512])
    rgi = arB.get([128, 512])
    rgr = arB.get([128, 512])
    rgt = arB.get([128, 512])
    rgg = arB.get([128, 512])
    arB.reset()
    pe0 = arB.get([128, 528])
    peA = arB.get([128, 528])
    peB = arB.get([128, 528])
    ptmp = arB.get([128, 512])
    xnf = arB.get([128, 1024])
    arB.reset()
    NFS = 3
    fue = [[arB.get([128, 516]) for i in range(2)] for k in range(NFS)]
    facc = [[arB.get([128, 512]) for i in range(2)] for k in range(NFS)]
    fgl = [arB.get([128, 512]) for k in range(NFS)]
    arB.reset()
    pidx = arB.get([128, NS * NPAGES])
    ptf = arB.get([128, NS * NPAGES])
    rgtail_s = arS.get([128, NS * 12])
    rghst_s = arS.get([128, NS * 4])
    ptail_s = arS.get([128, NS * 120])
    ftail_s = arS.get([128, NS * 96])
    lfp = arS.get([128, NPAGES, 8])
    totA = arS.get([128, NPAGES, 8])
    incl = arS.get([128, NPAGES, 8])
    biasp = arS.get([128, NPAGES, 8])
    biasn = arS.get([128, 8])
    tb1 = arS.get([128, NPAGES, 8])
    kraw = [arS.get([128, 512], BF16) for i in range(2)]
    vraw = [arS.get([128, 512], BF16) for i in range(2)]
    kTp = [arS.get([128, 4, 128], BF16) for i in range(2)]
    vpad = [arS.get([128, 4, 192], BF16) for i in range(2)]
    sbs = arS.get([128, 64])
    pts = [arS.get([128, 64], BF16) for i in range(2)]
    rgtail_so, rghst_so, ftail_so = rgtail_s, rghst_s, ftail_s

    banks = [st.enter_context(nc.psum_tensor(f"ps{i}", [128, 512], F32)) for i in range(8)]
    btok = [TK(f"bank{i}") for i in range(8)]
    rot = {"all": 0, "lo": 0}

    def ps_next(pool="all"):
        if pool == "all":
            i = rot["all"] % 8
            rot["all"] += 1
        else:
            i = rot["lo"] % 4
            rot["lo"] += 1
        return banks[i], btok[i]

    def OP(eng, fn, reads=(), writes=()):
        return S.op(eng, fn, [r for r in reads if r is not None], [w for w in writes if w is not None])

    def DMA(q, out, in_, key, reads=(), writes=(), nodeps=False, before=None):
        return S.dma(q, lambda e: e.dma_start(out=out, in_=in_), key, list(reads), list(writes), nodeps=nodeps, before=before)

    out_keys = set()
    wmark = {}
    wtok = [TK(f"wsl{i}") for i in range(3)]
    wtok_ids = {id(t): i for i, t in enumerate(wtok)}

    def OUT(dst, src, srcname=None):
        key = "o_" + (srcname if srcname is not None else "misc")
        out_keys.add(key)
        return S.dma("sp", lambda e: e.dma_start(out=dst, in_=src), key, [TK(srcname)] if srcname is not None else [], [])

    def MM(out_ap, pairs, reads, writes, start=True, stop=True):
        def fn(e):
            inst = None
            n = len(pairs)
            for i, (l, r) in enumerate(pairs):
                inst = e.matmul(out_ap, lhsT=l, rhs=r, start=(start and i == 0), stop=(stop and i == n - 1),
                                skip_group_check=True)
            return inst
        nd = OP("pe", fn, reads, writes)
        for r in reads:
            i = wtok_ids.get(id(r))
            if i is not None:
                wmark[i] = S.op("pool", None)
        return nd

    wrot = [0]

    def wload(src, reads=()):
        i = wrot[0] % 3
        wrot[0] += 1
        shp = list(src.shape)
        n = int(np.prod(shp[1:]))
        assert n <= 4096 and shp[0] == 128, shp
        dst = wsl[i][:, 0:n]
        if len(shp) == 3:
            dst = dst.rearrange("p (a b) -> p a b", b=shp[2])
        elif len(shp) == 4:
            dst = dst.rearrange("p (a b c) -> p a b c", b=shp[2], c=shp[3])
        before = wmark.get(i)
        if len(shp) == 4:
            for j in range(shp[2]):
                DMA("pool", dst[:, :, j, :], src[:, :, j, :], f"w{i}", reads=reads, writes=[wtok[i]], nodeps=(j > 0), before=before)
        else:
            DMA("pool", dst, src, f"w{i}", reads=reads, writes=[wtok[i]], before=before)
        return dst, wtok[i]

    gtok = [TK("gsl0")]

    def gload(row):
        i = 0
        DMA("sp", gsl[i][:, :], row.partition_broadcast(128), f"g{i}", writes=[gtok[i]])
        return gsl[i], gtok[i]

    def load_rows_T(dst, dtok, src, R):
        nt = (R + 127) // 128
        for t0 in range(0, nt, 4):
            t1 = min(nt, t0 + 4)
            r_end = min(R, t1 * 128)
            full = (r_end - t0 * 128) // 128
            if full:
                DMA("sp", rowbuf[:, 0:full, :], src[t0 * 128:(t0 + full) * 128, :].rearrange("(i p) c -> p i c", p=128), "rowbuf",
                    writes=[TK("rowbuf")])
            rem = (r_end - t0 * 128) % 128
            if rem:
                DMA("sp", rowbuf[0:rem, full, :], src[(t0 + full) * 128:r_end, :], "rowbuf", reads=[], writes=[TK("rowbuf")])
            bk, bt = ps_next()
            cols = 0
            for t in range(t0, t1):
                r = min(128, R - t * 128)
                OP("pe", lambda e, bk=bk, t=t, r=r, o=(t - t0) * 128, ti=t - t0: e.transpose(bk[:, o:o + r], rowbuf[0:r, ti, :], identf[0:r, 0:r]),
                   [TK("rowbuf"), TK("identf")], [bt])
                cols += r
            OP("act", lambda e, bk=bk, cols=cols, t0=t0: e.activation(out=dst[:, t0 * 128:t0 * 128 + cols], in_=bk[:, 0:cols], func=AF.Copy),
               [bt], [dtok])

    rorot = [0]

    def store_rows_T(src, stok, dstd, R, key="out"):
        nt = (R + 127) // 128
        for t in range(nt):
            r = min(128, R - t * 128)
            bk, bt = ps_next()
            OP("pe", lambda e, bk=bk, t=t, r=r: e.transpose(bk[0:r, 0:128], src[:, t * 128:t * 128 + r], identf[:, :]),
               [stok, TK("identf")], [bt])
            i = rorot[0] % 2
            rorot[0] += 1
            OP("act", lambda e, bk=bk, r=r, i=i: e.activation(out=rowo[i][0:r, :], in_=bk[0:r, 0:128], func=AF.Copy),
               [bt], [TK(f"rowo{i}")])
            OUT(dstd[t * 128:t * 128 + r, :], rowo[i][0:r, :], f"rowo{i}")

    def consts():
        for t_, nm in ((identb, "identb"), (identf, "identf"), (utri, "utri"), (buf_, "buf_"), (onesf, "onesf"),
                       (ones16, "ones16")):
            OP("pool", lambda e, t_=t_: e.memset(t_[:], 1.0), [], [TK(nm)])
        for t_, nm in ((identb, "identb"), (identf, "identf")):
            OP("pool", lambda e, t_=t_: e.affine_select(out=t_[:], in_=t_[:], pattern=[[-1, 128]], compare_op=ALU.is_equal,
                                                          fill=0.0, base=0, channel_multiplier=1), [TK(nm)], [TK(nm)])
        for t_, nm in ((utri, "utri"), (buf_, "buf_")):
            OP("pool", lambda e, t_=t_: e.affine_select(out=t_[:], in_=t_[:], pattern=[[1, 128]], compare_op=ALU.is_ge,
                                                          fill=0.0, base=0, channel_multiplier=-1), [TK(nm)], [TK(nm)])
        b3 = buf_[:].rearrange("p (b t) -> p b t", t=8)
        OP("pool", lambda e: e.affine_select(out=b3, in_=b3, pattern=[[-8, 16], [0, 8]], compare_op=ALU.is_ge,
                                              fill=0.0, base=0, channel_multiplier=1), [TK("buf_")], [TK("buf_")])
        OP("pool", lambda e: e.tensor_copy(out=maskc[:], in_=utri[:]), [TK("utri")], [TK("maskc")])
        OP("pool", lambda e: e.tensor_copy(out=maskb[:], in_=buf_[:]), [TK("buf_")], [TK("maskb")])
        OP("pool", lambda e: e.memset(onespad[:], 1.0), [], [TK("onespad")])
        OP("pool", lambda e: e.memset(onespad[:, 64:128], 0.0), [TK("onespad")], [TK("onespad")])
        OP("pool", lambda e: e.iota(invcnt[:], pattern=[[1, 16]], base=1, channel_multiplier=0,
                                    allow_small_or_imprecise_dtypes=True), [], [TK("invcnt")])
        OP("dve", lambda e: e.reciprocal(out=invcnt[:], in_=invcnt[:]), [TK("invcnt")], [TK("invcnt")])
        for l in range(DEPTH):
            load_rows_T(fpar[l], TK(f"fpar{l}"), ffn_par[l], 192)
        for li in range(NAB):
            load_rows_T(apar[li], TK(f"apar{li}"), ab_par[li], 32)
        DMA("sp", ptab[:, :], page_table.partition_broadcast(128), "ptab", writes=[TK("ptab")])
        OP("pool", lambda e: e.iota(pidx[:], pattern=[[0, NS * NPAGES]], base=0, channel_multiplier=1,
                                    allow_small_or_imprecise_dtypes=True), [], [TK("pidx")])
        OP("dve", lambda e: e.tensor_copy(out=ptf[:], in_=ptab[:]), [TK("ptab")], [TK("ptf")])
        OP("dve", lambda e: e.scalar_tensor_tensor(out=ptf[:], in0=ptf[:], scalar=128.0, in1=pidx[:], op0=ALU.mult, op1=ALU.add),
           [TK("ptf"), TK("pidx")], [TK("ptf")])
        OP("dve", lambda e: e.tensor_copy(out=idxall[:], in_=ptf[:]), [TK("ptf"), TK("ptab")], [TK("idxall")])

    def cview(ap2, ps_, col0, n):
        v = ap2[:, col0:col0 + n]
        if ps_.sample:
            v = v.rearrange("p (s t) -> p s t", t=DL)
        return v

    def eview(ext, ps_, Hh, col0, n, shift):
        if ps_.sample:
            e3 = ext[:, 0:NS * (Hh + DL)].rearrange("p (s t) -> p s t", t=Hh + DL)
            return e3[:, :, Hh + shift:Hh + shift + DL]
        return ext[:, Hh + col0 + shift:Hh + col0 + shift + n]

    def norm(ps_, grow, want_f32_last=False):
        g, gt_ = gload(grow)
        NT = ps_.NT
        for tt in range(NT):
            OP("act", lambda e, tt=tt: e.activation(out=xnb[tt % 2][:, :], in_=x[:, tt, :], func=AF.Square,
                                                      accum_out=ss[:, tt:tt + 1]),
               [TK(f"x{tt}"), TK("rs")], [TK(f"ss{tt}"), TK(f"xnb{tt % 2}")])
        OP("act", lambda e: e.activation(out=rs[:, 0:NT], in_=ss[:, 0:NT], func=AF.Sqrt, scale=1.0 / 1024, bias=EPS),
           [TK(f"ss{t}") for t in range(NT)], [TK("rs")])
        OP("dve", lambda e: e.reciprocal(out=rs[:, 0:NT], in_=rs[:, 0:NT]), [TK("rs")], [TK("rs")])
        for tt in range(NT):
            xb_ = xnb[tt % 2]
            xt_ = TK(f"xnb{tt % 2}")
            OP("dve", lambda e, tt=tt, xb_=xb_: e.scalar_tensor_tensor(out=xb_[:, :], in0=x[:, tt, :], scalar=rs[:, tt:tt + 1],
                                                                       in1=g[:, :], op0=ALU.mult, op1=ALU.mult),
               [TK(f"x{tt}"), TK("rs"), gt_], [xt_])
            if want_f32_last and tt == NT - 1:
                OP("dve", lambda e, tt=tt: e.scalar_tensor_tensor(out=xnf[:, :], in0=x[:, tt, :], scalar=rs[:, tt:tt + 1],
                                                                  in1=g[:, :], op0=ALU.mult, op1=ALU.mult),
                   [TK(f"x{tt}"), TK("rs"), gt_], [TK("xnf")])
            bk, bt = ps_next()
            pb = bk[:].bitcast(BF16)

            def tr(e, pb=pb, xb_=xb_):
                inst = None
                for kc in range(8):
                    inst = e.transpose(pb[:, kc * 128:(kc + 1) * 128], xb_[:, kc * 128:(kc + 1) * 128], identb[:])
                return inst
            OP("pe", tr, [xt_, TK("identb")], [bt])
            OP("act", lambda e, pb=pb, tt=tt: e.activation(out=xnT[:, :, tt * 128:(tt + 1) * 128],
                                                            in_=pb.rearrange("p (k t) -> p k t", t=128), func=AF.Copy),
               [bt], [TK(f"xnT{tt}")])

    def xn_tiles(ps_, col0, n):
        return [TK(f"xnT{t}") for t in range(col0 // 128, (col0 + n + 127) // 128)]

    def m_tiles(col0, n):
        return [TK(f"mT{t}") for t in range(col0 // 128, (col0 + n + 127) // 128)]

    def ab_setup(li):
        DMA("sp", gq[:, :], ab_q_gain[li:li + 1, :].partition_broadcast(128), "gq", writes=[TK("gq")])
        DMA("sp", gk[:, :], ab_k_gain[li:li + 1, :].partition_broadcast(128), "gk", writes=[TK("gk")])
        DMA("sp", bfb[:, :], ab_b_f[li:li + 1, :].partition_broadcast(128), "bfb", writes=[TK("bfb")])
        OP("dve", lambda e: e.tensor_reduce(out=gmx[:, 0:1], in_=gq[:, :], axis=AX.X, op=ALU.max, apply_absolute_value=True),
           [TK("gq")], [TK("gmx")])
        OP("dve", lambda e: e.tensor_reduce(out=gmx[:, 1:2], in_=gk[:, :], axis=AX.X, op=ALU.max, apply_absolute_value=True),
           [TK("gk"), TK("gmx")], [TK("gmx")])
        OP("dve", lambda e: e.scalar_tensor_tensor(out=negB[:, :], in0=gmx[:, 0:1], scalar=-8.0, in1=gmx[:, 1:2],
                                                   op0=ALU.mult, op1=ALU.mult), [TK("gmx")], [TK("negB")])
        ap_ = apar[li]
        OP("act", lambda e: e.activation(out=spn[:, 0:4], in_=ap_[:, 28:32], func=AF.Exp, scale=-1.0),
           [TK(f"apar{li}")], [TK("spn")])
        OP("act", lambda e: e.activation(out=spn[:, 0:4], in_=spn[:, 0:4], func=AF.Ln, bias=1.0), [TK("spn")], [TK("spn")])
        OP("dve", lambda e: e.tensor_scalar(out=spn[:, 4:8], in0=spn[:, 0:4], scalar1=-16.0, scalar2=None, op0=ALU.mult),
           [TK("spn")], [TK("spn")])
        OP("dve", lambda e: e.tensor_scalar(out=spn[:, 0:4], in0=spn[:, 0:4], scalar1=-8.0, scalar2=None, op0=ALU.mult),
           [TK("spn")], [TK("spn")])
        OP("pool", lambda e: e.memset(wabd[:], 0.0), [], [TK("wabd")])
        OP("pool", lambda e: e.memset(wxbd[:], 0.0), [], [TK("wxbd")])
        for hh in range(2):
            DMA("pool", wabd[hh * 64:(hh + 1) * 64, :, hh * 64:(hh + 1) * 64],
                ab_w_a[li].rearrange("(c two) i j -> two i c j", two=2)[hh], "wabd", reads=[], writes=[TK("wabd")])
            DMA("pool", wxbd[hh * 64:(hh + 1) * 64, :, hh * 64:(hh + 1) * 64],
                ab_w_x[li].rearrange("(c two) i j -> two i c j", two=2)[hh], "wxbd", reads=[], writes=[TK("wxbd")])

    def ab_tokmajor(ps_, li):
        Wv = ab_w_in[li].rearrange("(kc p) n -> p kc n", p=128)
        wq, wqt = wload(Wv[:, :, 0:512])
        wk, wkt = wload(Wv[:, :, 512:1024])
        wv, wvt = wload(Wv[:, :, 1024:1536])
        DMA("pool", wfb[:, :, :], Wv[:, :, 1536:1544], "wfb", writes=[TK("wfb")])
        wf, wft = wfb, TK("wfb")
        OP("pool", lambda e: e.memset(Vn[:, :, :, 64:128], 0.0), [TK("Vn")], [TK("Vn")])
        lfa, cka, cea = lfall[li], ckall[li], cendall[li]
        tl, tc, te = TK(f"lfall{li}"), TK(f"ckall{li}"), TK(f"cendall{li}")
        if ps_.sample or ps_.chunk == 0:
            OP("dve", lambda e: e.memset(cea[:, 0, :], 0.0), [], [te])
        for tt in range(ps_.NT):
            gt = ps_.gt0 + tt
            xt = [TK(f"xnT{tt}")]
            cols = slice(tt * 128, (tt + 1) * 128)
            pq, pqt = ps_next()
            pk, pkt = ps_next()
            pv, pvt = ps_next()
            pf, pft = ps_next()
            MM(pq[:, :], [(xnT[:, kc, cols], wq[:, kc, :]) for kc in range(8)], xt + [wqt], [pqt])
            MM(pk[:, :], [(xnT[:, kc, cols], wk[:, kc, :]) for kc in range(8)], xt + [wkt], [pkt])
            MM(pv[:, :], [(xnT[:, kc, cols], wv[:, kc, :]) for kc in range(8)], xt + [wvt], [pvt])
            MM(pf[:, 0:8], [(xnT[:, kc, cols], wf[:, kc, :]) for kc in range(8)], xt + [wft], [pft])
            for which, (pp, ppt) in enumerate(((pq, pqt), (pk, pkt))):
                so = which * 8
                OP("act", lambda e, pp=pp: e.activation(out=sq[:, :], in_=pp[:, :], func=AF.Square), [ppt], [TK("sq")])
                OP("dve", lambda e, so=so: e.tensor_reduce(out=ssq[:, so:so + 8], in_=sq[:, :].rearrange("p (h d) -> p h d", d=64),
                                                           axis=AX.X, op=ALU.add), [TK("sq")], [TK("ssq")])
                OP("act", lambda e, so=so: e.activation(out=rq[:, so:so + 8], in_=ssq[:, so:so + 8], func=AF.Sqrt,
                                                        scale=1.0 / 64, bias=EPS), [TK("ssq")], [TK("rq")])
                OP("dve", lambda e, so=so: e.reciprocal(out=rq[:, so:so + 8], in_=rq[:, so:so + 8]), [TK("rq")], [TK("rq")])
                OP("dve", lambda e, pp=pp, so=so: e.tensor_tensor(out=tmpf[:, :].rearrange("p (h d) -> p h d", d=64),
                                                                  in0=pp[:, :].rearrange("p (h d) -> p h d", d=64),
                                                                  in1=rq[:, so:so + 8].unsqueeze(2).broadcast_to([128, 8, 64]),
                                                                  op=ALU.mult), [ppt, TK("rq")], [TK("tmpf")])
                if which == 0:
                    OP("dve", lambda e: e.tensor_tensor(out=qb[:, :].rearrange("p (h d) -> p h d", d=64),
                                                        in0=tmpf[:, :].rearrange("p (h d) -> p h d", d=64),
                                                        in1=gq[:, :].unsqueeze(1).broadcast_to([128, 8, 64]), op=ALU.mult),
                       [TK("tmpf"), TK("gq")], [TK("qb")])
                else:
                    kf_ = kf[tt % 2]
                    kft = TK(f"kf{tt % 2}")
                    OP("dve", lambda e, kf_=kf_: e.tensor_tensor(out=kf_[:, :].rearrange("p (h d) -> p h d", d=64),
                                                                 in0=tmpf[:, :].rearrange("p (h d) -> p h d", d=64),
                                                                 in1=gk[:, :].unsqueeze(1).broadcast_to([128, 8, 64]), op=ALU.mult),
                       [TK("tmpf"), TK("gk")], [kft])
                    OP("act", lambda e, kf_=kf_: e.activation(out=kb[:, :], in_=kf_[:, :], func=AF.Copy), [kft], [TK("kb")])
                    OUT(ps_.k_out(li)[tt * 128:(tt + 1) * 128, :], kf_[:, :], f"kf{tt % 2}")
            vf_ = vf[tt % 2]
            vft = TK(f"vf{tt % 2}")
            OP("act", lambda e, vf_=vf_, pv=pv: e.activation(out=vf_[:, :], in_=pv[:, :], func=AF.Copy), [pvt], [vft])
            OUT(ps_.v_out(li)[tt * 128:(tt + 1) * 128, :], vf_[:, :], f"vf{tt % 2}")
            v4 = vf_[:, :].rearrange("p (c two d) -> p c two d", two=2, d=64)
            OP("pool", lambda e, tt=tt, v4=v4: e.tensor_copy(out=Vn[:, tt, :, 0:64], in_=v4[:, :, 0, :]), [vft], [TK("Vn")])
            OP("pool", lambda e, tt=tt, v4=v4: e.tensor_copy(out=Vn[:, tt, :, 128:192], in_=v4[:, :, 1, :]), [vft], [TK("Vn")])
            OP("dve", lambda e, pf=pf: e.tensor_tensor(out=t8[:, :], in0=pf[:, 0:8], in1=bfb[:, :], op=ALU.add),
               [pft, TK("bfb")], [TK("t8")])
            OP("act", lambda e: e.activation(out=e8[:, :], in_=t8[:, :], func=AF.Exp, scale=-1.0), [TK("t8")], [TK("e8")])
            OP("act", lambda e: e.activation(out=e8[:, :], in_=e8[:, :], func=AF.Ln, bias=1.0), [TK("e8")], [TK("e8")])
            OP("dve", lambda e, gt=gt: e.tensor_scalar(out=lfa[:, gt, :], in0=e8[:, :], scalar1=-1.0, scalar2=None, op0=ALU.mult),
               [TK("e8")], [tl])
            OUT(ps_.lf_out(li)[tt * 128:(tt + 1) * 128, :], lfa[:, gt, :], f"lfall{li}")
            pc, pct = ps_next()
            tri = buf_ if ps_.sample else utri
            MM(pc[:, 0:8], [(tri[:, :], lfa[:, gt, :])], [tl, TK("utri"), TK("buf_")], [pct])
            if ps_.sample:
                OP("dve", lambda e, pc=pc: e.scalar_tensor_tensor(out=biasn[:, :], in0=pc[:, 0:8], scalar=-1.0,
                                                                  in1=negB[:, 0:1].broadcast_to([128, 8]),
                                                                  op0=ALU.mult, op1=ALU.add), [pct, TK("negB")], [TK("biasn")])
            else:
                MM(pc[:, 8:16], [(onesf[:, :], lfa[:, gt, :])], [tl, TK("onesf")], [pct])
                OP("dve", lambda e, pc=pc, gt=gt: e.tensor_tensor(out=cka[:, gt, :], in0=pc[:, 0:8], in1=cea[:, gt, :], op=ALU.add),
                   [pct, te], [tc])
                OP("dve", lambda e, pc=pc, gt=gt: e.tensor_tensor(out=cea[:, gt + 1, :], in0=pc[:, 8:16], in1=cea[:, gt, :], op=ALU.add),
                   [pct, te], [te])
            for src, stok, dstT, dname in ((qb, TK("qb"), qT, "qT"), (kb, TK("kb"), kTn, "kTn")):
                bk, bt = ps_next()
                pb = bk[:].bitcast(BF16)

                def tr(e, pb=pb, src=src):
                    inst = None
                    for c in range(4):
                        inst = e.transpose(pb[:, c * 128:(c + 1) * 128], src[:, c * 128:(c + 1) * 128], identb[:])
                    return inst
                OP("pe", tr, [stok, TK("identb")], [bt])
                OP("act", lambda e, pb=pb, dstT=dstT, cols=cols: e.activation(out=dstT[:, :, cols],
                                                                              in_=pb[:, 0:512].rearrange("p (c t) -> p c t", t=128),
                                                                              func=AF.Copy), [bt], [TK(f"{dname}{tt}")])
        if (not ps_.sample) and ps_.chunk < NPASS - 1:
            i = ps_.chunk
            DMA("sp", kT_scr[li].rearrange("c p t -> p c t")[:, :, i * CH:(i + 1) * CH], kTn[:, :, :], f"kscr{li}",
                reads=[TK(f"kTn{t}") for t in range(8)], writes=[TK(f"kscr{li}")])
            DMA("sp", V_scr[li][:, i * 8:(i + 1) * 8], Vn[:, :, :, :], f"vscr{li}", reads=[TK("Vn")], writes=[TK(f"vscr{li}")])

    def ab_rg(ps_, li):
        Wv = ab_w_in[li].rearrange("(kc p) n -> p kc n", p=128)
        wxr, wxrt = wload(Wv[:, :, 1544:2056])
        wgt, wgtt = wload(Wv[:, :, 2056:2568])
        ap_ = apar[li]
        apt = TK(f"apar{li}")
        smp = ps_.sample
        for c in range(4):
            for (col0, n) in ps_.blocks:
                xt = xn_tiles(ps_, col0, n)
                if smp:
                    hv = rgtail_s[:, :].rearrange("p (s j c) -> p s j c", j=3, c=4)[:, :, :, c]
                    e3 = rge[:, 0:NS * 11].rearrange("p (s t) -> p s t", t=11)
                    OP("pool", lambda e, hv=hv, e3=e3: e.tensor_copy(out=e3[:, :, 0:3], in_=hv), [TK("rgtail_s")], [TK("rge")])
                else:
                    hv = rgtail[li][:, :].rearrange("p (j c) -> p j c", c=4)[:, :, c]
                    OP("pool", lambda e, hv=hv: e.tensor_copy(out=rge[:, 0:3], in_=hv), [TK(f"rgtail{li}")], [TK("rge")])
                px, pxt = ps_next()
                MM(px[:, 0:n], [(wxr[:, kc, c * 128:(c + 1) * 128], xnT[:, kc, col0:col0 + n]) for kc in range(8)],
                   xt + [wxrt], [pxt])
                pxv = px[:, 0:n].rearrange("p (s t) -> p s t", t=DL) if smp else px[:, 0:n]
                OP("act", lambda e, pxv=pxv, n=n: e.activation(out=eview(rge, ps_, 3, 0, n, 0), in_=pxv, func=AF.Copy),
                   [pxt], [TK("rge")])
                OP("act", lambda e, n=n, c=c: e.activation(out=cview(rgacc, ps_, 0, n), in_=eview(rge, ps_, 3, 0, n, 0),
                                                           func=AF.Identity, scale=ap_[:, 12 + c:13 + c], bias=ap_[:, 16 + c:17 + c]),
                   [TK("rge"), apt], [TK("rgacc")])
                for j in (2, 1, 0):
                    OP("dve", lambda e, n=n, c=c, j=j: e.scalar_tensor_tensor(
                        out=cview(rgacc, ps_, 0, n), in0=eview(rge, ps_, 3, 0, n, j - 3), scalar=ap_[:, j * 4 + c:j * 4 + c + 1],
                        in1=cview(rgacc, ps_, 0, n), op0=ALU.mult, op1=ALU.add), [TK("rge"), TK("rgacc"), apt], [TK("rgacc")])
                if smp:
                    to = rgtail_s[:, :].rearrange("p (s j c) -> p s j c", j=3, c=4)[:, :, :, c]
                    OP("pool", lambda e, e3=e3, to=to: e.tensor_copy(out=to, in_=e3[:, :, 8:11]), [TK("rge")], [TK("rgtail_s")])
                else:
                    OP("pool", lambda e, hv=hv, n=n: e.tensor_copy(out=hv, in_=rge[:, n:n + 3]), [TK("rge")], [TK(f"rgtail{li}")])
                OP("act", lambda e, n=n: e.activation(out=rgxb[:, 0:n], in_=rgacc[:, 0:n], func=AF.Copy), [TK("rgacc")], [TK("rgxb")])
                pr, prt = ps_next()
                pi, pit = ps_next()
                MM(pr[:, 0:n], [(wabd[:, c, :], rgxb[:, 0:n])], [TK("wabd"), TK("rgxb")], [prt])
                MM(pi[:, 0:n], [(wxbd[:, c, :], rgxb[:, 0:n])], [TK("wxbd"), TK("rgxb")], [pit])
                OP("act", lambda e, pr=pr, n=n, c=c: e.activation(out=rgr[:, 0:n], in_=pr[:, 0:n], func=AF.Sigmoid,
                                                                  bias=ap_[:, 20 + c:21 + c]), [prt, apt], [TK("rgr")])
                OP("act", lambda e, pi=pi, n=n, c=c: e.activation(out=rgi[:, 0:n], in_=pi[:, 0:n], func=AF.Sigmoid,
                                                                  bias=ap_[:, 24 + c:25 + c]), [pit, apt], [TK("rgi")])
                OP("act", lambda e, n=n, c=c: e.activation(out=rga[:, 0:n], in_=rgr[:, 0:n], func=AF.Exp, scale=spn[:, c:c + 1]),
                   [TK("rgr"), TK("spn")], [TK("rga")])
                OP("act", lambda e, n=n, c=c: e.activation(out=rgt[:, 0:n], in_=rgr[:, 0:n], func=AF.Exp, scale=spn[:, 4 + c:5 + c]),
                   [TK("rgr"), TK("spn")], [TK("rgt")])
                OP("act", lambda e, n=n: e.activation(out=rgt[:, 0:n], in_=rgt[:, 0:n], func=AF.Sqrt, scale=-1.0, bias=1.0),
                   [TK("rgt")], [TK("rgt")])
                OP("dve", lambda e, n=n: e.tensor_tensor(out=rgi[:, 0:n], in0=rgi[:, 0:n], in1=rgt[:, 0:n], op=ALU.mult),
                   [TK("rgi"), TK("rgt")], [TK("rgi")])
                OP("dve", lambda e, n=n: e.tensor_tensor(out=rgi[:, 0:n], in0=rgi[:, 0:n], in1=rgacc[:, 0:n], op=ALU.mult),
                   [TK("rgi"), TK("rgacc")], [TK("rgi")])
                if smp:
                    a3 = rga[:, 0:n].rearrange("p (s t) -> p s t", t=DL)
                    i3 = rgi[:, 0:n].rearrange("p (s t) -> p s t", t=DL)
                    hs = rghst_s[:, :].rearrange("p (s c) -> p s c", c=4)[:, :, c:c + 1]
                    OP("dve", lambda e, a3=a3, hs=hs: e.tensor_tensor(out=rgt[:, 0:NS].unsqueeze(2), in0=a3[:, :, 0:1], in1=hs, op=ALU.mult),
                       [TK("rga"), TK("rghst_s")], [TK("rgt")])
                    OP("dve", lambda e, i3=i3: e.tensor_tensor(out=i3[:, :, 0:1], in0=i3[:, :, 0:1], in1=rgt[:, 0:NS].unsqueeze(2), op=ALU.add),
                       [TK("rgi"), TK("rgt")], [TK("rgi")])
                    OP("dve", lambda e, a3=a3: e.memset(a3[:, :, 0:1], 0.0), [TK("rga")], [TK("rga")])
                    OP("dve", lambda e, n=n: e.tensor_tensor_scan(out=rgacc[:, 0:n], data0=rga[:, 0:n], data1=rgi[:, 0:n], initial=0.0,
                                                                  op0=ALU.mult, op1=ALU.add), [TK("rga"), TK("rgi")], [TK("rgacc")])
                    h3 = rgacc[:, 0:n].rearrange("p (s t) -> p s t", t=DL)
                    OP("pool", lambda e, h3=h3, hs=hs: e.tensor_copy(out=hs, in_=h3[:, :, DL - 1:DL]), [TK("rgacc")], [TK("rghst_s")])
                else:
                    OP("dve", lambda e, c=c, n=n: e.tensor_tensor_scan(out=rgacc[:, 0:n], data0=rga[:, 0:n], data1=rgi[:, 0:n],
                                                                       initial=rghst[li][:, c:c + 1], op0=ALU.mult, op1=ALU.add),
                       [TK("rga"), TK("rgi"), TK(f"rghst{li}")], [TK("rgacc")])
                    OP("pool", lambda e, c=c, n=n: e.tensor_copy(out=rghst[li][:, c:c + 1], in_=rgacc[:, n - 1:n]),
                       [TK("rgacc")], [TK(f"rghst{li}")])
                pg, pgt = ps_next()
                MM(pg[:, 0:n], [(wgt[:, kc, c * 128:(c + 1) * 128], xnT[:, kc, col0:col0 + n]) for kc in range(8)],
                   xt + [wgtt], [pgt])
                OP("act", lambda e, pg=pg, n=n: e.activation(out=rgg[:, 0:n], in_=pg[:, 0:n], func=AF.Gelu_apprx_tanh), [pgt], [TK("rgg")])
                OP("dve", lambda e, n=n, col0=col0, c=c: e.tensor_tensor(out=mT[:, 4 + c, col0:col0 + n], in0=rgacc[:, 0:n],
                                                                         in1=rgg[:, 0:n], op=ALU.mult),
                   [TK("rgacc"), TK("rgg")], m_tiles(col0, n))

    ND = [((banks[4], btok[4]), (banks[5], btok[5])), ((banks[6], btok[6]), (banks[7], btok[7]))]
    ptrot = [0]

    def attn_prompt(ps_, li):
        i = ps_.chunk
        cka, cea = ckall[li], cendall[li]
        tc, te = TK(f"ckall{li}"), TK(f"cendall{li}")
        for b in range(2):
            gmid = i * 8 + 4 * b + 2
            OP("dve", lambda e, b=b, gmid=gmid: e.tensor_tensor(out=crefB[:, b, :], in0=cea[:, gmid, :],
                                                                in1=negB[:, 0:1].broadcast_to([128, 8]), op=ALU.add),
               [te, TK("negB")], [TK("crefB")])
            ng = i * 8 + 4 * b + 4
            OP("dve", lambda e, b=b, ng=ng: e.scalar_tensor_tensor(out=biasall[:, b, 0:ng, :], in0=cka[:, 0:ng, :], scalar=-1.0,
                                                                   in1=crefB[:, b:b + 1, :].broadcast_to([128, ng, 8]),
                                                                   op0=ALU.mult, op1=ALU.add), [tc, TK("crefB")], [TK("biasall")])
        qts = [TK(f"qT{t}") for t in range(8)]
        LA = 3
        for c in range(4):
            kc_, vc_ = kcache[0], vcache[0]
            kct, vct = TK("kcache0"), TK("vcache0")
            if i > 0:
                DMA("sp", kc_[:, 0:i * CH], kT_scr[li][c][:, 0:i * CH], "kc0", reads=[TK(f"kscr{li}")], writes=[kct])
                DMA("sp", vc_[:, 0:i * 8, :], V_scr[li][:, 0:i * 8, c, :], "vc0", reads=[TK(f"vscr{li}")], writes=[vct])
            items = []
            for b in range(2):
                ktiles = [(True, g) for g in range(i * 8)] + [(False, t) for t in range(4 * b + 4)]
                nk = len(ktiles)
                for ki, (past, t) in enumerate(ktiles):
                    for hh in range(2):
                        items.append((b, ki, nk, past, t, hh))
            staged = {}

            def stageA(it, c=c):
                b, ki, nk, past, t, hh = it
                g = t if past else i * 8 + t
                diag = (not past) and t >= 4 * b
                qlo = (t - 4 * b) * 128 if diag else 0
                h = 2 * c + hh
                r0 = hh * 64
                pS, pSt = ps_next("lo")
                if past:
                    lhsT = kc_[r0:r0 + 64, t * 128:(t + 1) * 128]
                    lr = [kct]
                else:
                    lhsT = kTn[r0:r0 + 64, c, t * 128:(t + 1) * 128]
                    lr = [TK(f"kTn{t}")]
                MM(pS[:, qlo:512], [(lhsT, qT[r0:r0 + 64, c, b * 512 + qlo:b * 512 + 512])], lr + qts[4 * b:4 * b + 4], [pSt])
                pi_ = ptrot[0] % NPT
                ptrot[0] += 1
                PT, PTt = ptb[pi_], TK(f"ptb{pi_}")
                OP("act", lambda e, pS=pS, PT=PT, qlo=qlo, b=b, g=g, h=h: e.activation(
                    out=PT[:, qlo:512], in_=pS[:, qlo:512], func=AF.Exp, scale=0.125, bias=biasall[:, b, g, h:h + 1]),
                   [pSt, TK("biasall")], [PTt])
                if diag:
                    OP("pool", lambda e, PT=PT, qlo=qlo: e.tensor_tensor(out=PT[:, qlo:qlo + 128], in0=PT[:, qlo:qlo + 128],
                                                                          in1=maskc[:, :], op=ALU.mult), [PTt, TK("maskc")], [PTt])
                staged[it] = (PT, PTt, qlo)

            def stageB(it, c=c):
                b, ki, nk, past, t, hh = it
                PT, PTt, qlo = staged.pop(it)
                r0 = hh * 64
                (Nb, Nt), (Db, Dt) = ND[(c * 2 + b) % 2]
                if past:
                    vl = vc_[:, t, r0:r0 + 128]
                    vr = [vct]
                else:
                    vl = Vn[:, t, c, r0:r0 + 128]
                    vr = [TK("Vn")]
                first = (ki == 0 and hh == 0)
                last = (ki == nk - 1 and hh == 1)
                MM(Nb[:, qlo:512], [(vl, PT[:, qlo:512])], vr + [PTt], [Nt], start=first, stop=last)
                MM(Db[:, qlo:512], [(onespad[:, r0:r0 + 128], PT[:, qlo:512])], [TK("onespad"), PTt], [Dt], start=first, stop=last)
                if last:
                    OP("dve", lambda e, Db=Db: e.reciprocal(out=rden[:, :], in_=Db[:, :]), [Dt], [TK("rden")])
                    OP("dve", lambda e, Nb=Nb, c=c, b=b: e.tensor_tensor(out=mT[:, c, b * 512:(b + 1) * 512], in0=Nb[:, :], in1=rden[:, :],
                                                                         op=ALU.mult), [Nt, TK("rden")], m_tiles(b * 512, 512))
            n_it = len(items)
            for k in range(n_it + LA):
                if k < n_it:
                    stageA(items[k])
                if k >= LA:
                    stageB(items[k - LA])

    grot3 = [0]

    def attn_sample(ps_, li):
        (Nb, Nt), (Db, Dt) = ND[0]
        qts = [TK("qT0")]
        for i in range(2):
            OP("pool", lambda e, i=i: e.memset(vpad[i][:, :, 64:128], 0.0), [TK(f"vpad{i}")], [TK(f"vpad{i}")])
        for c in range(4):
            for hh in range(2):
                h = 2 * c + hh
                r0 = hh * 64
                pS, pSt = ps_next("lo")
                MM(pS[:, 0:128], [(kTn[r0:r0 + 64, c, 0:128], qT[r0:r0 + 64, c, 0:128])], [TK("kTn0")] + qts, [pSt])
                pi_ = ptrot[0] % NPT
                ptrot[0] += 1
                PT, PTt = ptb[pi_], TK(f"ptb{pi_}")
                OP("act", lambda e, pS=pS, PT=PT, h=h: e.activation(out=PT[:, 0:128], in_=pS[:, 0:128], func=AF.Exp, scale=0.125,
                                                                    bias=biasn[:, h:h + 1]), [pSt, TK("biasn")], [PTt])
                OP("pool", lambda e, PT=PT: e.tensor_tensor(out=PT[:, 0:128], in0=PT[:, 0:128], in1=maskb[:, :], op=ALU.mult),
                   [PTt, TK("maskb")], [PTt])
                MM(Nb[:, c * 128:(c + 1) * 128], [(Vn[:, 0, c, r0:r0 + 128], PT[:, 0:128])], [TK("Vn"), PTt], [Nt], start=(hh == 0 and c == 0), stop=False)
                MM(Db[:, c * 128:(c + 1) * 128], [(onespad[:, r0:r0 + 128], PT[:, 0:128])], [TK("onespad"), PTt], [Dt], start=(hh == 0 and c == 0), stop=False)
        ck_rows = cache_k.rearrange("l r d -> (l r) d")
        cv_rows = cache_v.rearrange("l r d -> (l r) d")
        cl_rows = cache_lf.rearrange("l r d -> (l r) d")
        eo_kv = li * NPG * 128 * 512
        eo_lf = li * NPG * 128 * 8
        for s in range(NS):
            for pg in range(NPAGES):
                col = s * NPAGES + pg
                S.dma("pool", lambda e, pg=pg, col=col: e.indirect_dma_start(
                    out=lfp[:, pg, :], out_offset=None, in_=cl_rows,
                    in_offset=bass.IndirectOffsetOnAxis(ap=idxall[:, col:col + 1], axis=0), element_offset=eo_lf), "lfp", [TK("idxall")], [TK("lfp")], nodeps=(pg > 0))
            pc, pct = ps_next("lo")
            lf2 = lfp[:, :, :].rearrange("p a b -> p (a b)")
            W = NPAGES * 8
            MM(pc[:, 0:W], [(utri[:, :], lf2)], [TK("lfp"), TK("utri")], [pct])
            MM(pc[:, 128:128 + W], [(onesf[:, :], lf2)], [TK("lfp"), TK("onesf")], [pct])
            OP("act", lambda e, pc=pc: e.activation(out=totA[:, :, :].rearrange("p a b -> p (a b)"), in_=pc[:, 128:128 + W], func=AF.Copy),
               [pct], [TK("totA")])
            for h in range(8):
                OP("dve", lambda e, h=h: e.tensor_tensor_scan(out=incl[:, :, h], data0=ones16[:, 0:NPAGES], data1=totA[:, :, h],
                                                              initial=0.0, op0=ALU.mult, op1=ALU.add),
                   [TK("totA"), TK("ones16")], [TK("incl")])
            OP("dve", lambda e, pc=pc: e.tensor_tensor(out=tb1[:, :, :].rearrange("p a b -> p (a b)"),
                                                       in0=totA[:, :, :].rearrange("p a b -> p (a b)"), in1=pc[:, 0:W], op=ALU.subtract),
               [TK("totA"), pct], [TK("tb1")])
            OP("dve", lambda e: e.tensor_tensor(out=tb1[:, :, :], in0=tb1[:, :, :], in1=incl[:, :, :], op=ALU.subtract),
               [TK("tb1"), TK("incl")], [TK("tb1")])
            OP("dve", lambda e: e.tensor_tensor(out=tb1[:, :, :], in0=tb1[:, :, :],
                                                in1=incl[:, NPAGES - 1:NPAGES, :].broadcast_to([128, NPAGES, 8]), op=ALU.add),
               [TK("tb1"), TK("incl")], [TK("tb1")])
            OP("dve", lambda e: e.tensor_scalar(out=biasp[:, :, :], in0=tb1[:, :, :], scalar1=negB[:, 0:1], scalar2=None, op0=ALU.add),
               [TK("tb1"), TK("negB")], [TK("biasp")])
            for pg in range(NPAGES):
                col = s * NPAGES + pg
                gi = grot3[0] % 2
                grot3[0] += 1
                kr, krt = kraw[gi], TK(f"kraw{gi}")
                vr_, vrt = vraw[gi], TK(f"vraw{gi}")
                S.dma("pool", lambda e, kr=kr, col=col: e.indirect_dma_start(
                    out=kr[:, :], out_offset=None, in_=ck_rows,
                    in_offset=bass.IndirectOffsetOnAxis(ap=idxall[:, col:col + 1], axis=0), element_offset=eo_kv), f"kraw{gi}", [TK("idxall")], [krt])
                S.dma("pool", lambda e, vr_=vr_, col=col: e.indirect_dma_start(
                    out=vr_[:, :], out_offset=None, in_=cv_rows,
                    in_offset=bass.IndirectOffsetOnAxis(ap=idxall[:, col:col + 1], axis=0), element_offset=eo_kv), f"vraw{gi}", [TK("idxall")], [vrt])
                bi = col % 2
                bk, bt = ps_next("lo")
                pb = bk[:].bitcast(BF16)

                def tr(e, pb=pb, kr=kr):
                    inst = None
                    for c in range(4):
                        inst = e.transpose(pb[:, c * 128:(c + 1) * 128], kr[:, c * 128:(c + 1) * 128], identb[:])
                    return inst
                OP("pe", tr, [krt, TK("identb")], [bt])
                OP("act", lambda e, pb=pb, bi=bi: e.activation(out=kTp[bi][:, :, :], in_=pb[:, 0:512].rearrange("p (c t) -> p c t", t=128),
                                                               func=AF.Copy), [bt], [TK(f"kTp{bi}")])
                v4 = vr_[:, :].rearrange("p (c two d) -> p c two d", two=2, d=64)
                OP("act", lambda e, bi=bi, v4=v4: e.activation(out=vpad[bi][:, :, 0:64], in_=v4[:, :, 0, :], func=AF.Copy), [vrt], [TK(f"vpad{bi}")])
                OP("act", lambda e, bi=bi, v4=v4: e.activation(out=vpad[bi][:, :, 128:192], in_=v4[:, :, 1, :], func=AF.Copy), [vrt], [TK(f"vpad{bi}")])
                pS, pSt = ps_next("lo")

                def qk(e, pS=pS, bi=bi, s=s):
                    inst = None
                    for h in range(8):
                        c, r0 = h // 2, (h % 2) * 64
                        inst = e.matmul(pS[:, h * 8:(h + 1) * 8], lhsT=kTp[bi][r0:r0 + 64, c, :], rhs=qT[r0:r0 + 64, c, s * 8:(s + 1) * 8],
                                        start=True, stop=True, skip_group_check=True)
                    return inst
                OP("pe", qk, [TK(f"kTp{bi}")] + qts, [pSt])
                OP("dve", lambda e, pS=pS, pg=pg: e.scalar_tensor_tensor(
                    out=sbs[:, :].rearrange("p (h q) -> p h q", q=8), in0=pS[:, 0:64].rearrange("p (h q) -> p h q", q=8), scalar=0.125,
                    in1=biasp[:, pg, :].unsqueeze(2).broadcast_to([128, 8, 8]), op0=ALU.mult, op1=ALU.add),
                   [pSt, TK("biasp")], [TK("sbs")])
                OP("act", lambda e, bi=bi: e.activation(out=pts[bi][:, :], in_=sbs[:, :], func=AF.Exp), [TK("sbs")], [TK(f"pts{bi}")])
                lastpg = (pg == NPAGES - 1)

                def pvm(e, bi=bi, s=s, lastpg=lastpg):
                    inst = None
                    for h in range(8):
                        c, r0 = h // 2, (h % 2) * 64
                        o = c * 128 + s * 8
                        st_ = lastpg and (h % 2 == 1)
                        e.matmul(Nb[:, o:o + 8], lhsT=vpad[bi][:, c, r0:r0 + 128], rhs=pts[bi][:, h * 8:(h + 1) * 8],
                                 start=False, stop=st_, skip_group_check=True)
                        inst = e.matmul(Db[:, o:o + 8], lhsT=onespad[:, r0:r0 + 128], rhs=pts[bi][:, h * 8:(h + 1) * 8],
                                        start=False, stop=st_, skip_group_check=True)
                    return inst
                OP("pe", pvm, [TK(f"vpad{bi}"), TK(f"pts{bi}"), TK("onespad")], [Nt, Dt])
        OP("dve", lambda e: e.reciprocal(out=rden[:, :], in_=Db[:, :]), [Dt], [TK("rden")])
        OP("dve", lambda e: e.tensor_tensor(out=mT[:, 0:4, 0:128], in0=Nb[:, :].rearrange("p (c t) -> p c t", t=128),
                                            in1=rden[:, :].rearrange("p (c t) -> p c t", t=128), op=ALU.mult),
           [Nt, TK("rden")], m_tiles(0, 128))

    def out_proj_add(ps_, Wd, nk):
        Wv = Wd.rearrange("(kc p) n -> p kc n", p=128)
        ws = []
        for k0 in range(0, nk, 4):
            ws.append(wload(Wv[:, k0:k0 + 4, :]))
        for tt in range(ps_.NT):
            cols = slice(tt * 128, (tt + 1) * 128)
            for half in range(2):
                bk, bt = ps_next()
                pairs = [(mT[:, kc, cols], ws[kc // 4][0][:, kc % 4, half * 512:(half + 1) * 512]) for kc in range(nk)]
                MM(bk[:, :], pairs, [TK(f"mT{tt}")] + [w[1] for w in ws], [bt])
                OP("dve", lambda e, bk=bk, tt=tt, half=half: e.tensor_tensor(out=x[:, tt, half * 512:(half + 1) * 512],
                                                                             in0=x[:, tt, half * 512:(half + 1) * 512], in1=bk[:, :], op=ALU.add),
                   [bt, TK(f"x{tt}")], [TK(f"x{tt}")])

    def ab_mixer(ps_, li):
        ab_setup(li)
        if ps_.sample:
            load_rows_T(rgtail_s, TK("rgtail_s"), st_rg_conv[li], NS * 12)
            load_rows_T(rghst_s, TK("rghst_s"), st_rg_h[li], NS * 4)
        ab_tokmajor(ps_, li)
        ab_rg(ps_, li)
        if ps_.sample:
            attn_sample(ps_, li)
            store_rows_T(rgtail_so, TK("rgtail_s"), rg_conv_s[li], NS * 12)
            store_rows_T(rghst_so, TK("rghst_s"), rg_h_s[li], NS * 4)
        else:
            attn_prompt(ps_, li)
            if ps_.chunk == NPASS - 1:
                store_rows_T(rgtail[li], TK(f"rgtail{li}"), rg_conv_p[li], 12)
                store_rows_T(rghst[li], TK(f"rghst{li}"), rg_h_p[li], 4)
        out_proj_add(ps_, ab_w_out[li], 8)

    def pool_mixer(ps_, li):
        smp = ps_.sample
        if smp:
            load_rows_T(ptail_s, TK("ptail_s"), st_pool[li], NS * 120)
            OUT(pool_s[li][:, 0:7, :], st_pool[li].rearrange("(s j kc) p -> s j (kc p)", j=15, kc=8)[:, 8:15, :])
            for s_ in range(NS):
                OUT(pool_s[li][s_, 7:15, :], xnf[s_ * 8:(s_ + 1) * 8, :], "xnf")
        elif ps_.chunk == NPASS - 1:
            OUT(pool_p[li][:, :], xnf[113:128, :], "xnf")
        pw, pwt = wload(pool_w[li].rearrange("g (kk p) n -> p (g kk) n", p=128))
        psc, psct = gload(pool_scale[li:li + 1, :])

        def ev(buf, lo, hi):
            if smp:
                return buf[:, 0:NS * 23].rearrange("p (s t) -> p s t", t=23)[:, :, lo:hi]
            return buf[:, lo:hi]
        for g, w in enumerate(POOL_WINDOWS):
            for kk in range(2):
                kc = 2 * g + kk
                for (col0, n) in ps_.blocks:
                    L = DL if smp else n
                    E = 15 + L
                    if smp:
                        hv = ptail_s[:, :].rearrange("p (s j k) -> p s j k", j=15, k=8)[:, :, :, kc]
                        OP("pool", lambda e, hv=hv: e.tensor_copy(out=ev(pe0, 0, 15), in_=hv), [TK("ptail_s")], [TK("pe0")])
                        dsrc = xnT[:, kc, 0:n].rearrange("p (s t) -> p s t", t=DL)
                    else:
                        hv = ptail[li][:, :].rearrange("p (j k) -> p j k", k=8)[:, :, kc]
                        OP("pool", lambda e, hv=hv: e.tensor_copy(out=pe0[:, 0:15], in_=hv), [TK(f"ptail{li}")], [TK("pe0")])
                        dsrc = xnT[:, kc, col0:col0 + n]
                    OP("pool", lambda e, dsrc=dsrc, E=E: e.tensor_copy(out=ev(pe0, 15, E), in_=dsrc), xn_tiles(ps_, col0, n), [TK("pe0")])
                    src, stok = pe0, TK("pe0")
                    step = 1
                    k = 0
                    while step < w:
                        dst, dtok = (peA, TK("peA")) if k % 2 == 0 else (peB, TK("peB"))
                        lo = 2 * step - 1
                        OP("dve", lambda e, src=src, dst=dst, lo=lo, step=step, E=E: e.tensor_tensor(
                            out=ev(dst, lo, E), in0=ev(src, lo, E), in1=ev(src, lo - step, E - step), op=ALU.add), [stok], [dtok])
                        src, stok = dst, dtok
                        step *= 2
                        k += 1
                    dcols = mT[:, kc, 0:n].rearrange("p (s t) -> p s t", t=DL) if smp else mT[:, kc, col0:col0 + n]
                    OP("dve", lambda e, src=src, dcols=dcols, w=w, E=E: e.scalar_tensor_tensor(
                        out=dcols, in0=ev(src, 15, E), scalar=1.0 / w, in1=ev(pe0, 15, E), op0=ALU.mult, op1=ALU.subtract),
                       [stok, TK("pe0")], m_tiles(col0, n))
                    if (not smp) and ps_.chunk == 0 and col0 == 0:
                        OP("dve", lambda e, src=src, w=w: e.tensor_tensor(out=ptmp[:, 0:w - 1], in0=src[:, 15:15 + w - 1],
                                                                          in1=invcnt[:, 0:w - 1], op=ALU.mult), [stok, TK("invcnt")], [TK("ptmp")])
                        OP("dve", lambda e, kc=kc, w=w: e.tensor_tensor(out=mT[:, kc, 0:w - 1], in0=ptmp[:, 0:w - 1], in1=pe0[:, 15:15 + w - 1],
                                                                        op=ALU.subtract), [TK("ptmp"), TK("pe0")], m_tiles(0, 128))
                    if not smp:
                        OP("pool", lambda e, hv=hv, n=n: e.tensor_copy(out=hv, in_=pe0[:, n:n + 15]), [TK("pe0")], [TK(f"ptail{li}")])
        for tt in range(ps_.NT):
            cols = slice(tt * 128, (tt + 1) * 128)
            for half in range(2):
                bk, bt = ps_next()
                for gg in range(2):
                    g = half * 2 + gg
                    MM(bk[:, gg * 256:(gg + 1) * 256], [(mT[:, 2 * g + kk, cols], pw[:, 2 * g + kk, :]) for kk in range(2)],
                       [TK(f"mT{tt}"), pwt], [bt])
                OP("dve", lambda e, bk=bk, half=half: e.tensor_tensor(out=ptmp[:, :], in0=bk[:, :], in1=psc[:, half * 512:(half + 1) * 512],
                                                                      op=ALU.mult), [bt, psct], [TK("ptmp")])
                OP("pool", lambda e, tt=tt, half=half: e.tensor_tensor(out=x[:, tt, half * 512:(half + 1) * 512],
                                                                       in0=x[:, tt, half * 512:(half + 1) * 512], in1=ptmp[:, :], op=ALU.add),
                   [TK("ptmp"), TK(f"x{tt}")], [TK(f"x{tt}")])

    frot = [0]

    def ffn(ps_, l):
        smp = ps_.sample
        fp_ = fpar[l]
        fpt = TK(f"fpar{l}")
        Wu = ffn_w_up[l].rearrange("(kc p) (two f) -> p kc two f", p=128, two=2)
        if smp:
            load_rows_T(ftail_s, TK("ftail_s"), st_ffn[l], NS * 96)
        last = (not smp) and ps_.chunk == NPASS - 1
        wcur = None
        for fc in range(24):
            if fc % 2 == 0:
                wcur = wload(Wu[:, :, :, fc * 128:(fc + 2) * 128])
            wu, wut = wcur
            fo = (fc % 2) * 128
            for (col0, n) in ps_.blocks:
                xt = xn_tiles(ps_, col0, n)
                fs = frot[0] % NFS
                frot[0] += 1
                for br in range(2):
                    ch = br * 24 + fc
                    ue = fue[fs][br]
                    uet = TK(f"fue{fs}_{br}")
                    ac = facc[fs][br]
                    act_ = TK(f"facc{fs}_{br}")
                    if smp:
                        hv = ftail_s[:, :].rearrange("p (s j f) -> p s j f", j=2, f=48)[:, :, :, ch]
                        e3 = ue[:, 0:NS * 10].rearrange("p (s t) -> p s t", t=10)
                        OP("pool", lambda e, hv=hv, e3=e3: e.tensor_copy(out=e3[:, :, 0:2], in_=hv), [TK("ftail_s")], [uet])
                    else:
                        hv = ftail[l][:, :].rearrange("p (j f) -> p j f", f=48)[:, :, ch]
                        OP("pool", lambda e, hv=hv, ue=ue: e.tensor_copy(out=ue[:, 0:2], in_=hv), [TK(f"ftail{l}_{ch}")], [uet])
                    bk, bt = ps_next()
                    MM(bk[:, 0:n], [(wu[:, kc, br, fo:fo + 128], xnT[:, kc, col0:col0 + n]) for kc in range(8)], xt + [wut], [bt])
                    bv = bk[:, 0:n].rearrange("p (s t) -> p s t", t=DL) if smp else bk[:, 0:n]
                    OP("act", lambda e, bv=bv, ue=ue, n=n: e.activation(out=eview(ue, ps_, 2, 0, n, 0), in_=bv, func=AF.Copy), [bt], [uet])
                    OP("act", lambda e, ue=ue, ac=ac, n=n, ch=ch: e.activation(
                        out=cview(ac, ps_, 0, n), in_=eview(ue, ps_, 2, 0, n, 0), func=AF.Identity,
                        scale=fp_[:, 96 + ch:97 + ch], bias=fp_[:, 144 + ch:145 + ch]), [uet, fpt], [act_])
                    for j in (1, 0):
                        OP("dve", lambda e, ue=ue, ac=ac, n=n, ch=ch, j=j: e.scalar_tensor_tensor(
                            out=cview(ac, ps_, 0, n), in0=eview(ue, ps_, 2, 0, n, j - 2), scalar=fp_[:, j * 48 + ch:j * 48 + ch + 1],
                            in1=cview(ac, ps_, 0, n), op0=ALU.mult, op1=ALU.add), [uet, act_, fpt], [act_])
                    if smp:
                        OP("pool", lambda e, e3=e3, hv=hv: e.tensor_copy(out=hv, in_=e3[:, :, 8:10]), [uet], [TK("ftail_s")])
                    else:
                        OP("pool", lambda e, hv=hv, ue=ue, n=n: e.tensor_copy(out=hv, in_=ue[:, n:n + 2]), [uet], [TK(f"ftail{l}_{ch}")])
                OP("act", lambda e, n=n, fs=fs: e.activation(out=fgl[fs][:, 0:n], in_=facc[fs][0][:, 0:n], func=AF.Gelu_apprx_tanh),
                   [TK(f"facc{fs}_0")], [TK(f"fgl{fs}")])
                OP("pool", lambda e, n=n, fc=fc, col0=col0, fs=fs: e.tensor_tensor(out=hT[:, fc, col0:col0 + n], in0=fgl[fs][:, 0:n],
                                                                                   in1=facc[fs][1][:, 0:n], op=ALU.mult),
                   [TK(f"fgl{fs}"), TK(f"facc{fs}_1")], [TK(f"hT{fc}_{col0}")])
        if smp:
            store_rows_T(ftail_s, TK("ftail_s"), ffn_conv_s[l], NS * 96)
        elif last:
            OP("pool", lambda e, l=l: e.tensor_copy(out=ftail[l][:, 0:1], in_=ftail[l][:, 0:1]),
               [TK(f"ftail{l}_{ch}") for ch in range(48)], [TK(f"ftail{l}")])
            store_rows_T(ftail[l], TK(f"ftail{l}"), ffn_conv_p[l], 96)
        Wd = ffn_w_down[l].rearrange("(fc p) n -> p fc n", p=128)
        for sg in range(6):
            wd, wdt = wload(Wd[:, sg * 4:(sg + 1) * 4, :])
            for tt in range(ps_.NT):
                cols = slice(tt * 128, (tt + 1) * 128)
                c0 = (tt * 128) // 512 * 512 if not smp else 0
                for half in range(2):
                    bk, bt = ps_next()
                    MM(bk[:, :], [(hT[:, sg * 4 + j, cols], wd[:, j, half * 512:(half + 1) * 512]) for j in range(4)],
                       [TK(f"hT{sg * 4 + j}_{c0}") for j in range(4)] + [wdt], [bt])
                    OP("dve", lambda e, bk=bk, tt=tt, half=half: e.tensor_tensor(out=x[:, tt, half * 512:(half + 1) * 512],
                                                                                 in0=x[:, tt, half * 512:(half + 1) * 512], in1=bk[:, :], op=ALU.add),
                       [bt, TK(f"x{tt}")], [TK(f"x{tt}")])

    def run_pass(ps_):
        for tt in range(ps_.NT):
            DMA("sp", x[:, tt, :], ps_.x_src[tt * 128:(tt + 1) * 128, :], f"xin{tt}", writes=[TK(f"x{tt}")])
        for l in range(DEPTH):
            li = l // 2
            S.fence()
            if l % 2 == 0:
                norm(ps_, norm_mix[l:l + 1, :])
                ab_mixer(ps_, li)
            else:
                norm(ps_, norm_mix[l:l + 1, :], want_f32_last=True)
                pool_mixer(ps_, li)
            S.fence()
            norm(ps_, norm_ffn[l:l + 1, :])
            ffn(ps_, l)
        for tt in range(ps_.NT):
            OUT(ps_.y_dst[tt * 128:(tt + 1) * 128, :], x[:, tt, :], f"x{tt}")

    consts()
    for li in range(NAB):
        OP("pool", lambda e, li=li: e.memset(rgtail[li][:], 0.0), [], [TK(f"rgtail{li}")])
        OP("pool", lambda e, li=li: e.memset(rghst[li][:], 0.0), [], [TK(f"rghst{li}")])
    for li in range(NPL):
        OP("pool", lambda e, li=li: e.memset(ptail[li][:], 0.0), [], [TK(f"ptail{li}")])
    for l in range(DEPTH):
        OP("pool", lambda e, l=l: e.memset(ftail[l][:], 0.0), [], [TK(f"ftail{l}_{ch}") for ch in range(48)])
    passes = []
    for i in range(NPASS):
        p_ = Pass()
        p_.sample = False
        p_.chunk = i
        p_.T = CH
        p_.L = CH
        p_.NT = CH // 128
        p_.gt0 = i * (CH // 128)
        p_.blocks = [(0, 512), (512, 512)]
        p_.x_src = xp[i * CH:(i + 1) * CH, :]
        p_.y_dst = y_p[i * CH:(i + 1) * CH, :]
        p_.k_out = lambda li, i=i: k_p[li][i * CH:(i + 1) * CH, :]
        p_.v_out = lambda li, i=i: v_p[li][i * CH:(i + 1) * CH, :]
        p_.lf_out = lambda li, i=i: logf_p[li][i * CH:(i + 1) * CH, :]
        passes.append(p_)
    sp_ = Pass()
    sp_.sample = True
    sp_.chunk = 0
    sp_.T = 128
    sp_.L = DL
    sp_.NT = 1
    sp_.gt0 = 0
    sp_.blocks = [(0, 128)]
    sp_.x_src = xs
    sp_.y_dst = y_s
    sp_.k_out = lambda li: k_s[li]
    sp_.v_out = lambda li: v_s[li]
    sp_.lf_out = lambda li: logf_s[li]
    for p_ in passes:
        run_pass(p_)
    S.fence()
    run_pass(sp_)
    print("total recorded", S.count)
    print("SBUF arenas: A", arA.hi, "B", arB.hi, "S", arS.hi, "ops", {e: len(S.ops[e]) for e in ENGS})
    S.emit(nc, st, final_keys=sorted(out_keys))
    st.close()
    return nc


_NC_CACHE = {}


def _get_nc(cfg_key):
    if cfg_key not in _NC_CACHE:
        _NC_CACHE[cfg_key] = build(Cfg(*cfg_key))
    return _NC_CACHE[cfg_key]


def kernel(x_prompt, x_sample, cache_k, cache_v, cache_logf, state_rg_h, state_rg_conv, state_pool,
           state_ffn_conv, page_table, norm_mix, norm_ffn, ab_w_in, ab_b_f, ab_q_gain, ab_k_gain,
           ab_conv_w, ab_conv_b, ab_w_a, ab_b_a, ab_w_x, ab_b_x, ab_lambda, ab_w_out, pool_w, pool_scale,
           ffn_w_up, ffn_conv_w, ffn_conv_b, ffn_w_down):
    f = lambda a: np.ascontiguousarray(np.asarray(a))
    x_prompt, x_sample = f(x_prompt), f(x_sample)
    B, SEQ, D = x_prompt.shape
    DB, DL, _ = x_sample.shape
    DEPTH = norm_mix.shape[0]
    NAB = (DEPTH + 1) // 2
    NPL = DEPTH // 2
    NPG = cache_k.shape[1]
    NPAGES = page_table.shape[1]
    n = 8
    NS = DB // n
    cfg_key = (SEQ, DEPTH, NPG, NPAGES)
    nc = _get_nc(cfg_key)
    ck = f(cache_k).reshape(NAB, NPG * 128, 512)
    cv = f(cache_v).reshape(NAB, NPG * 128, 512)
    cl = f(cache_logf).reshape(NAB, NPG * 128, 8)
    ab_par = np.concatenate([f(ab_conv_w).reshape(NAB, 16, 128), f(ab_conv_b).reshape(NAB, 4, 128),
                             f(ab_b_a).reshape(NAB, 4, 128), f(ab_b_x).reshape(NAB, 4, 128),
                             f(ab_lambda).reshape(NAB, 4, 128)], axis=1)
    ffn_par = np.concatenate([f(ffn_conv_w).reshape(DEPTH, 144, 128), f(ffn_conv_b).reshape(DEPTH, 48, 128)], axis=1)
    pw = f(pool_w) if NPL else np.zeros((1, 4, 256, 256), np.float32)
    psc = f(pool_scale) if NPL else np.zeros((1, 1024), np.float32)
    shared = {
        "cache_k": ck, "cache_v": cv, "cache_lf": cl,
        "norm_mix": f(norm_mix), "norm_ffn": f(norm_ffn), "ab_w_in": f(ab_w_in), "ab_b_f": f(ab_b_f),
        "ab_q_gain": f(ab_q_gain), "ab_k_gain": f(ab_k_gain), "ab_par": ab_par, "ab_w_a": f(ab_w_a), "ab_w_x": f(ab_w_x),
        "ab_w_out": f(ab_w_out), "pool_w": pw, "pool_scale": psc, "ffn_w_up": f(ffn_w_up), "ffn_par": ffn_par,
        "ffn_w_down": f(ffn_w_down),
    }
    srh, src_, spl, sff, ptb_ = f(state_rg_h), f(state_rg_conv), f(state_pool), f(state_ffn_conv), f(page_table)
    in_maps = []
    for c in range(n):
        sl = slice(c * NS, (c + 1) * NS)
        m = dict(shared)
        m["xp"] = x_prompt[c % B]
        m["xs"] = x_sample[sl].reshape(NS * DL, D)
        m["st_rg_h"] = srh[:, sl].reshape(NAB, NS * 4, 128)
        m["st_rg_conv"] = src_[:, sl].reshape(NAB, NS * 12, 128)
        m["st_pool"] = spl[:, sl].reshape(NPL, NS * 120, 128) if NPL else np.zeros((1, NS * 120, 128), np.float32)
        m["st_ffn"] = sff[:, sl].reshape(DEPTH, NS * 96, 128)
        m["page_table"] = ptb_[sl].reshape(1, NS * NPAGES).astype(np.int32)
        in_maps.append(m)
    res = run_bass_kernel_spmd(nc, in_maps, core_ids=list(range(n)))
    R = res.results
    cat = lambda name, ax: np.concatenate([R[c][name] for c in range(n)], axis=ax)
    y_prompt = np.stack([R[b]["y_p"] for b in range(B)])
    y_sample = cat("y_s", 0).reshape(DB, DL, D)
    k_p = np.stack([R[b]["k_p"] for b in range(B)], axis=1).reshape(NAB, B, SEQ, 8, 64)
    v_p = np.stack([R[b]["v_p"] for b in range(B)], axis=1).reshape(NAB, B, SEQ, 8, 64)
    logf_p = np.stack([R[b]["logf_p"] for b in range(B)], axis=1)
    rg_h_p = np.stack([R[b]["rg_h_p"] for b in range(B)], axis=1).reshape(NAB, B, 512)
    rg_conv_p = np.stack([R[b]["rg_conv_p"] for b in range(B)], axis=1).reshape(NAB, B, 3, 512)
    pool_p = np.stack([R[b]["pool_p"] for b in range(B)], axis=1)[:NPL]
    ffn_conv_p = np.stack([R[b]["ffn_conv_p"] for b in range(B)], axis=1).reshape(DEPTH, B, 2, 6144)
    k_s = cat("k_s", 1).reshape(NAB, DB, DL, 8, 64)
    v_s = cat("v_s", 1).reshape(NAB, DB, DL, 8, 64)
    logf_s = cat("logf_s", 1).reshape(NAB, DB, DL, 8)
    rg_h_s = cat("rg_h_s", 1).reshape(NAB, DB, 512)
    rg_conv_s = cat("rg_conv_s", 1).reshape(NAB, DB, 3, 512)
    pool_s = cat("pool_s", 1)[:NPL]
    ffn_conv_s = cat("ffn_conv_s", 1).reshape(DEPTH, DB, 2, 6144)
    return (y_prompt, y_sample, k_p, v_p, logf_p, rg_h_p, rg_conv_p, pool_p, ffn_conv_p,
            k_s, v_s, logf_s, rg_h_s, rg_conv_s, pool_s, ffn_conv_s)
```
